# Optimizing a Trainium2 kernel written in Bass

```python
import jax, jax.numpy as jnp
from jax import lax
import numpy as np

D_MODEL = 2048
BATCH = 16
SEQ = 2048
DEPTH = 4

N_MEM = 256
MIX_WIDTH = D_MODEL
HGRN_WIDTH = MIX_WIDTH // 2
CONV_WIDTH = MIX_WIDTH - HGRN_WIDTH
HGRN_HEAD_DIM = 128
HGRN_HEADS = HGRN_WIDTH // HGRN_HEAD_DIM
CONV_GROUP_DIM = 128
CONV_K = 3
CHUNK = 64
XATTN_HEADS = 4
XATTN_HEAD_DIM = D_MODEL // XATTN_HEADS
D_FF = 4 * D_MODEL
EPS = 1e-6
IN_WIDTH = 4 * HGRN_WIDTH + 3 * CONV_WIDTH
SPLITS = [int(s) for s in np.cumsum([HGRN_WIDTH] * 4 + [CONV_WIDTH] * 2)]

kernel_name = "hymba_hgrn2_shortconv_memxattn_trunk"


def rms_norm(x, g):
    xf = x.astype(jnp.float32)
    y = xf * lax.rsqrt(jnp.mean(jnp.square(xf), axis=-1, keepdims=True) + EPS)
    return (y * g.astype(jnp.float32)).astype(x.dtype)


def group_rms_norm(x, g, group):
    shp = x.shape
    xf = x.astype(jnp.float32).reshape(*shp[:-1], shp[-1] // group, group)
    y = xf * lax.rsqrt(jnp.mean(jnp.square(xf), axis=-1, keepdims=True) + EPS)
    return (y.reshape(shp) * g.astype(jnp.float32)).astype(x.dtype)


def hgrn2_mix(q_in, f_in, i_in, lb):
    bsz, seq, _ = q_in.shape
    n_chunks = seq // CHUNK
    f32 = jnp.float32

    def heads(t):
        t = t.astype(f32).reshape(bsz, n_chunks, CHUNK, HGRN_HEADS, HGRN_HEAD_DIM)
        return t.transpose(1, 0, 3, 2, 4)

    f = lb + (1.0 - lb) * jax.nn.sigmoid(f_in.astype(f32))
    q = heads(jax.nn.silu(q_in.astype(f32)))
    k = heads(1.0 - f)
    logf = heads(jnp.log(f))
    v = heads(i_in)
    causal = jnp.tril(jnp.ones((CHUNK, CHUNK), dtype=bool))

    def step(state, inp):
        qc, kc, vc, gc = inp
        b = jnp.cumsum(gc, axis=2)
        o_inter = jnp.einsum('bhck,bhkv->bhcv', qc * jnp.exp(b), state)
        rel = b[:, :, :, None, :] - b[:, :, None, :, :]
        decay = jnp.where(causal[:, :, None], jnp.exp(jnp.minimum(rel, 0.0)), 0.0)
        scores = jnp.einsum('bhtk,bhtsk,bhsk->bhts', qc, decay, kc)
        o = o_inter + jnp.einsum('bhts,bhsv->bhtv', scores, vc)
        b_last = b[:, :, -1:, :]
        state = (jnp.exp(b_last[:, :, 0, :])[..., None] * state
                 + jnp.einsum('bhck,bhcv->bhkv', kc * jnp.exp(b_last - b), vc))
        return state, o

    s0 = jnp.zeros((bsz, HGRN_HEADS, HGRN_HEAD_DIM, HGRN_HEAD_DIM), f32)
    _, o = lax.scan(step, s0, (q, k, v, logf))
    return o.transpose(1, 0, 3, 2, 4).reshape(bsz, seq, HGRN_WIDTH)


def causal_depthwise_conv(u, w):
    return lax.conv_general_dilated(
        u, w[:, None, :].astype(u.dtype), window_strides=(1,),
        padding=[(CONV_K - 1, 0)], dimension_numbers=('NWC', 'WIO', 'NWC'),
        feature_group_count=u.shape[-1])


def setup_inputs(seed: int = 0) -> dict:
    key = jax.random.key(seed)
    ks = jax.random.split(key, 24)
    f32 = jnp.float32

    def nrm(k, shape, scale):
        return jax.random.normal(k, shape, f32) * scale

    def gain(k, dim):
        return 1.0 + 0.02 * jax.random.normal(k, (DEPTH, dim), f32)

    return {
        "x": nrm(ks[0], (BATCH, SEQ, D_MODEL), 1.0),
        "mem": nrm(ks[1], (BATCH, N_MEM, D_MODEL), 1.0),
        "g_mix_pre": gain(ks[2], D_MODEL),
        "w_in": nrm(ks[3], (DEPTH, D_MODEL, IN_WIDTH), D_MODEL ** -0.5),
        "hgrn_lb_logits": nrm(ks[4], (DEPTH, HGRN_WIDTH), 1.0),
        "hgrn_norm_g": gain(ks[5], HGRN_WIDTH),
        "conv_w": nrm(ks[6], (DEPTH, CONV_K, CONV_WIDTH), CONV_K ** -0.5),
        "conv_norm_g": gain(ks[7], CONV_WIDTH),
        "w_mix_out": nrm(ks[8], (DEPTH, MIX_WIDTH, D_MODEL), MIX_WIDTH ** -0.5),
        "g_mix_post": gain(ks[9], D_MODEL),
        "g_x_pre": gain(ks[10], D_MODEL),
        "g_mem": gain(ks[11], D_MODEL),
        "w_q": nrm(ks[12], (DEPTH, D_MODEL, D_MODEL), D_MODEL ** -0.5),
        "w_k": nrm(ks[13], (DEPTH, D_MODEL, D_MODEL), D_MODEL ** -0.5),
        "w_v": nrm(ks[14], (DEPTH, D_MODEL, D_MODEL), D_MODEL ** -0.5),
        "w_xo": nrm(ks[15], (DEPTH, D_MODEL, D_MODEL), D_MODEL ** -0.5),
        "g_x_post": gain(ks[16], D_MODEL),
        "g_mlp_pre": gain(ks[17], D_MODEL),
        "w_up": nrm(ks[18], (DEPTH, D_MODEL, D_FF), D_MODEL ** -0.5),
        "w_down": nrm(ks[19], (DEPTH, D_FF, D_MODEL), D_FF ** -0.5),
        "g_mlp_post": gain(ks[20], D_MODEL),
    }


def reference(x, mem, g_mix_pre, w_in, hgrn_lb_logits, hgrn_norm_g, conv_w, conv_norm_g,
              w_mix_out, g_mix_post, g_x_pre, g_mem, w_q, w_k, w_v, w_xo, g_x_post,
              g_mlp_pre, w_up, w_down, g_mlp_post):
    bsz, seq, _ = x.shape
    n_mem = mem.shape[1]
    lb_cum = jnp.cumsum(jax.nn.softmax(hgrn_lb_logits.astype(jnp.float32), axis=0), axis=0)
    lower_bounds = lb_cum - lb_cum[0:1]

    for l in range(DEPTH):
        h = rms_norm(x, g_mix_pre[l])
        proj = h @ w_in[l]
        q_in, f_in, i_in, g_in, c_b, c_c, c_h = jnp.split(proj, SPLITS, axis=-1)
        o_h = hgrn2_mix(q_in, f_in, i_in, lower_bounds[l]).astype(x.dtype)
        o_h = group_rms_norm(o_h, hgrn_norm_g[l], HGRN_HEAD_DIM) * jax.nn.silu(g_in)
        y_c = causal_depthwise_conv(c_c * c_h, conv_w[l])
        o_c = group_rms_norm(c_b * y_c, conv_norm_g[l], CONV_GROUP_DIM)
        mix = jnp.concatenate([o_h, o_c], axis=-1) @ w_mix_out[l]
        x = x + rms_norm(mix, g_mix_post[l])

        h = rms_norm(x, g_x_pre[l])
        m = rms_norm(mem, g_mem[l])
        q = (h @ w_q[l]).reshape(bsz, seq, XATTN_HEADS, XATTN_HEAD_DIM)
        k = (m @ w_k[l]).reshape(bsz, n_mem, XATTN_HEADS, XATTN_HEAD_DIM)
        v = (m @ w_v[l]).reshape(bsz, n_mem, XATTN_HEADS, XATTN_HEAD_DIM)
        s = jnp.einsum('bshd,bnhd->bhsn', q, k).astype(jnp.float32) * (XATTN_HEAD_DIM ** -0.5)
        p = jax.nn.softmax(s, axis=-1).astype(x.dtype)
        a = jnp.einsum('bhsn,bnhd->bshd', p, v).reshape(bsz, seq, D_MODEL) @ w_xo[l]
        x = x + rms_norm(a, g_x_post[l])

        h = rms_norm(x, g_mlp_pre[l])
        u = jnp.square(jax.nn.relu(h @ w_up[l])) @ w_down[l]
        x = x + rms_norm(u, g_mlp_post[l])
    return x
```

```python
import contextlib
import numpy as np
import concourse.bass as bass
import concourse.mybir as mybir
from concourse.bass_utils import run_bass_kernel_spmd

F32 = mybir.dt.float32
BF16 = mybir.dt.bfloat16
AF = mybir.ActivationFunctionType
ALU = mybir.AluOpType
AX = mybir.AxisListType

D = 2048
T = 512
NST = 4
KC = 16
NMEM = 256
DFF = 8192
EPS = 1e-6
IN_W = 7168


class Res:
    __slots__ = ("name", "lw", "rd")

    def __init__(self, name=""):
        self.name = name
        self.lw = None
        self.rd = []


class DSem:
    __slots__ = ("sem", "count")

    def __init__(self, sem):
        self.sem = sem
        self.count = 0


class Sched:
    ENGS = ("pe", "act", "dve", "pool", "sp")

    def __init__(self, sems):
        self.sem = sems
        self.ops = {e: [] for e in self.ENGS}
        self.cnt = {e: 0 for e in self.ENGS}
        self.waited = {e: {} for e in self.ENGS}
        self.nwaits = 0

    def op(self, eng, fn, reads=(), writes=(), dsem=None, selfsync=True):
        need = {}

        def add(tok):
            if tok is None:
                return
            sem, val, src = tok
            if src == eng and not selfsync:
                return
            k = sem.num
            if k not in need or need[k][1] < val:
                need[k] = (sem, val)

        for r in reads:
            add(r.lw)
        for w in writes:
            add(w.lw)
            for t in w.rd:
                add(t)
        waits = []
        wd = self.waited[eng]
        for k, (sem, val) in need.items():
            if wd.get(k, 0) >= val:
                continue
            wd[k] = val
            waits.append((sem, val))
        self.nwaits += len(waits)
        if dsem is not None:
            dsem.count += 16
            tok = (dsem.sem, dsem.count, None)
            inc = (dsem.sem, 16)
        else:
            self.cnt[eng] += 1
            tok = (self.sem[eng], self.cnt[eng], eng)
            inc = (self.sem[eng], 1)
        self.ops[eng].append((waits, fn, inc))
        for r in reads:
            r.rd.append(tok)
        for w in writes:
            w.lw = tok
            w.rd = []
        return tok

    def handoff(self, olds, news):
        toks = []
        for o in olds:
            if o.lw is not None:
                toks.append(o.lw)
            toks.extend(o.rd)
        best = {}
        for t in toks:
            k = t[0].num
            if k not in best or best[k][1] < t[1]:
                best[k] = t
        for n in news:
            n.rd.extend(best.values())

    def replay(self, block, final_waits):
        def run(name, e):
            for waits, fn, inc in self.ops[name]:
                for sem, val in waits:
                    e.wait_ge(sem, val)
                ins = fn(e)
                ins.then_inc(inc[0], inc[1])
            if name == "sp":
                for sem, val in final_waits:
                    e.wait_ge(sem, val)

        @block.tensor
        def _(e):
            run("pe", e)

        @block.scalar
        def _(e):
            run("act", e)

        @block.vector
        def _(e):
            run("dve", e)

        @block.gpsimd
        def _(e):
            run("pool", e)

        @block.sync
        def _(e):
            run("sp", e)


def build(n_layers=4, n_tiles=8, tiles_per_seq=4, phases=("mix", "attn", "mlp"), n_seq=2):
    L = n_layers
    nc = bass.Bass("TRN2", target_bir_lowering=False)
    es = contextlib.ExitStack()

    def dram(name, shape, dt, kind="ExternalInput"):
        return nc.dram_tensor(name, list(shape), dt, kind=kind).ap()

    x_d = dram("x", [n_tiles * T, D], F32)
    mem_d = dram("mem", [n_seq * NMEM, D], F32)
    out_d = dram("out", [n_tiles * T, D], F32, kind="ExternalOutput")
    w_in_d = dram("w_in", [L, D, IN_W], F32)
    w_mo_d = dram("w_mix_out", [L, D, D], F32)
    w_q_d = dram("w_q", [L, D, D], F32)
    w_k_d = dram("w_k", [L, D, D], F32)
    w_v_d = dram("w_v", [L, D, D], F32)
    w_xo_d = dram("w_xo", [L, D, D], F32)
    w_up_d = dram("w_up", [L, D, DFF], F32)
    w_dn_d = dram("w_down", [L, DFF, D], F32)
    gnames = ["g_mix_pre", "g_mix_post", "g_x_pre", "g_mem", "g_x_post", "g_mlp_pre", "g_mlp_post"]
    g_d = {n: dram(n, [L, D], F32) for n in gnames}
    lbl_d = dram("lbT", [128, L, 8], F32)
    hg_d = dram("hgT", [128, L, 8], F32)
    cg_d = dram("cgT", [128, L, 8], F32)
    cw_d = dram("cwT", [128, L, 3, 8], F32)
    ident_d = dram("ident", [128, 128], BF16)
    masks_d = dram("masks", [128, 6, T], BF16)
    kT_scr = dram("kT_scr", [L, 128, KC, n_seq * NMEM], BF16, kind="Internal")
    v_scr = dram("v_scr", [L, 128, 2 * n_seq, D], BF16, kind="Internal")
    s_scr = dram("s_scr", [L, 128, 8 * 128], F32, kind="Internal")

    def sb(name, shape, dt):
        return es.enter_context(nc.sbuf_tensor(name, list(shape), dt))

    def newsem(name):
        return es.enter_context(nc.semaphore(name))

    S = Sched({e: newsem("sem_" + e) for e in Sched.ENGS})

    xs = sb("xs", [128, NST, D], F32)
    yb = sb("yb", [128, NST * D], F32)
    hT = sb("hT", [128, KC, T], BF16)
    Bb = sb("Bb", [128, 32, T], BF16)
    NW = 3
    wbuf = [sb("wb%d" % i, [128, KC, 512], BF16) for i in range(NW)]
    gbuf = [sb("gb%d" % i, [128, D], F32) for i in range(2)]
    Sst = sb("Sst", [128, 8, 128], F32)
    xn = [sb("xn%d" % i, [128, D], BF16) for i in range(2)]
    rtb = sb("rtb", [128, 2 * T], F32)
    rt = [rtb[:, 0:T], rtb[:, T:2 * T]]
    junk = sb("junk", [128, D], BF16)[:]
    ssq = sb("ssq", [128, NST, 4], F32)
    sst2 = sb("sst2", [128, NST], F32)
    rpost = sb("rpost", [128, NST], F32)
    lnp = sb("lnp", [128, NST], F32)
    ident = sb("ident_s", [128, 128], BF16)
    ones = sb("ones_s", [128, 128], F32)
    masks = sb("masks_s", [128, 6, T], BF16)
    el16 = sb("el16", [128, 32], F32)
    bl64 = sb("bl64", [128, 8], F32)
    bl32 = sb("bl32", [128, 16], F32)
    lbt = sb("lbt", [128, L, 8], F32)
    lb = sb("lb", [128, L, 8], F32)
    oml = sb("oml", [128, L, 8], F32)
    lbe = sb("lbe", [128, L, 8], F32)
    lbm = sb("lbm", [128, 8], F32)
    hgs = sb("hgs", [128, L, 8], F32)
    cgs = sb("cgs", [128, L, 8], F32)
    cws = sb("cws", [128, L, 3, 8], F32)
    tails = sb("tails", [128, L, 8, 2], F32)
    ss = sb("ss", [128, NST, 4], F32)
    sstot = sb("sstot", [128, NST], F32)
    lnv = sb("lnv", [128, NST], F32)
    rstd = sb("rstd", [128, NST], F32)
    eps_t = sb("eps_t", [128, 1], F32)
    amax = sb("amax", [128, 4], F32)
    anm = sb("anm", [128, 4], F32)
    asum = sb("asum", [128, 4], F32)
    arinv = sb("arinv", [128, 4], F32)
    elast = sb("elast", [128, 8], F32)

    psum = [es.enter_context(nc.psum_tensor("ps%d" % i, [128, 512], F32)) for i in range(8)]

    R = {}

    def res(name):
        if name not in R:
            R[name] = Res(name)
        return R[name]

    rt_r = [res("rt0"), res("rt1")]
    xs_r = [res("xs%d" % i) for i in range(NST)]
    yb_r = [res("yb%d" % i) for i in range(NST)]
    hT_r = [res("hT%d" % i) for i in range(NST)]
    oc_r = [res("ocat%d" % i) for i in range(NST)]
    ps_r = [res("ps%d" % i) for i in range(8)]
    wb_r = [res("wb%d" % i) for i in range(NW)]
    wb_s = [DSem(newsem("wbs%d" % i)) for i in range(NW)]
    gb_r = [res("gb%d" % i) for i in range(2)]
    gb_s = [DSem(newsem("gbs%d" % i)) for i in range(2)]
    xn_r = [res("xn%d" % i) for i in range(2)]
    x_ld = DSem(newsem("x_ld"))
    x_st = DSem(newsem("x_st"))
    misc_s = DSem(newsem("misc_s"))
    kv_s = DSem(newsem("kv_s"))
    vv_s = DSem(newsem("vv_s"))
    vscr_s = DSem(newsem("vscr_s"))
    sst_s = DSem(newsem("sst_s"))
    scr_s = DSem(newsem("scr_s"))

    state = {"bank": 0, "wb": 0, "gb": 0}

    free_banks = list(range(8))

    def next_bank():
        assert free_banks, "out of PSUM banks"
        return free_banks.pop(0)

    def rel_bank(b):
        assert b not in free_banks
        free_banks.append(b)

    def next_wb():
        b = state["wb"]
        state["wb"] = (b + 1) % NW
        return b

    def psbf(b):
        return psum[b][:].bitcast(BF16)

    class Arena:
        def __init__(self, base_ap_f32, nbytes, tag):
            self.ap = base_ap_f32
            self.n = nbytes
            self.off = 0
            self.tag = tag
            self.items = []

        def alloc(self, name, shape_free, dt):
            esz = 4 if dt == F32 else 2
            nel = int(np.prod(shape_free))
            nb = nel * esz
            nb4 = (nb + 3) // 4 * 4
            assert self.off + nb4 <= self.n, (self.tag, name, self.off, nb4, self.n)
            v = self.ap[:, self.off // 4:(self.off + nb4) // 4]
            if dt != F32:
                v = v.bitcast(dt)
            if len(shape_free) == 2:
                v = v.rearrange("p (a b) -> p a b", a=shape_free[0])
            elif len(shape_free) == 3:
                v = v.rearrange("p (a b c) -> p a b c", a=shape_free[0], b=shape_free[1])
            self.off += nb4
            r = Res(self.tag + "." + name)
            self.items.append(r)
            return v, r

    def dma(eng, out, in_, reads, writes, dsem):
        return S.op(eng, lambda e, o=out, i=in_: e.dma_start(out=o, in_=i), reads=reads, writes=writes, dsem=dsem)

    c_r = res("consts")
    dma("sp", ident[:], ident_d, [], [c_r], misc_s)
    dma("sp", masks[:], masks_d, [], [c_r], misc_s)
    dma("sp", lbt[:], lbl_d, [], [c_r], misc_s)
    dma("sp", hgs[:], hg_d, [], [c_r], misc_s)
    dma("sp", cgs[:], cg_d, [], [c_r], misc_s)
    dma("sp", cws[:], cw_d, [], [c_r], misc_s)
    S.op("dve", lambda e: e.memset(ones[:], 1.0), writes=[res("ones")])
    S.op("dve", lambda e: e.memset(eps_t[:], EPS), writes=[res("eps")])
    eps_r = res("eps")
    ones_r = res("ones")

    lb_r = res("lb")
    lbv = lbt[:].rearrange("p l h -> p h l")
    S.op("dve", lambda e: e.tensor_reduce(out=lbm[:], in_=lbv, axis=AX.X, op=ALU.max), reads=[c_r], writes=[lb_r])
    for l in range(L):
        S.op("dve", lambda e, l=l: e.tensor_tensor(out=lbe[:, l, :], in0=lbt[:, l, :], in1=lbm[:], op=ALU.subtract),
             reads=[c_r, lb_r], writes=[lb_r])
    S.op("act", lambda e: e.activation(out=lbe[:], in_=lbe[:], func=AF.Exp), reads=[lb_r], writes=[lb_r])
    S.op("dve", lambda e: e.tensor_reduce(out=lbm[:], in_=lbe[:].rearrange("p l h -> p h l"), axis=AX.X, op=ALU.add),
         reads=[lb_r], writes=[lb_r])
    S.op("dve", lambda e: e.reciprocal(out=lbm[:], in_=lbm[:]), reads=[lb_r], writes=[lb_r])
    S.op("dve", lambda e: e.memset(lb[:, 0, :], 0.0), reads=[lb_r], writes=[lb_r])
    for l in range(1, L):
        S.op("dve", lambda e, l=l: e.tensor_tensor(out=lbe[:, l, :], in0=lbe[:, l, :], in1=lbm[:], op=ALU.mult),
             reads=[lb_r], writes=[lb_r])
        S.op("dve", lambda e, l=l: e.tensor_tensor(out=lb[:, l, :], in0=lb[:, l - 1, :], in1=lbe[:, l, :], op=ALU.add),
             reads=[lb_r], writes=[lb_r])
    S.op("dve", lambda e: e.tensor_scalar(out=oml[:], in0=lb[:], scalar1=-1.0, scalar2=1.0, op0=ALU.mult, op1=ALU.add),
         reads=[lb_r], writes=[lb_r])

    def load_gb(gname, l):
        i = state["gb"]
        state["gb"] = (i + 1) % 2
        dma("sp", gbuf[i][:], g_d[gname][l:l + 1, :].partition_broadcast(128), [], [gb_r[i]], gb_s[i])
        return i

    ssq_r = [res("ssq%d" % i) for i in range(NST)]
    junk_r = [res("junk%d" % i) for i in range(4)]
    spre_r = [res("spre%d" % i) for i in range(NST)]
    rpost_r = [res("rpost%d" % i) for i in range(NST)]
    rpre_r = [res("rpre%d" % i) for i in range(NST)]

    def boundary(post, pre, src=xs, src_r=xs_r, dst=hT, dst_r=hT_r):
        ybv_ = yb[:].rearrange("p (s d) -> p s d", s=NST)
        gpre = load_gb(pre[0], pre[1]) if pre is not None else None

        def post_stats(st):
            gi, fused = post
            if fused:
                S.op("dve", lambda e: e.tensor_reduce(out=sst2[:, st:st + 1], in_=ssq[:, st, :], axis=AX.X, op=ALU.add),
                     reads=[ssq_r[st]], writes=[rpost_r[st]])
                ssrc = sst2[:, st:st + 1]
            else:
                S.op("act", lambda e: e.activation(out=junk, in_=ybv_[:, st, :], func=AF.Square,
                                                   accum_out=sst2[:, st:st + 1]), reads=[yb_r[st]], writes=[rpost_r[st]] + junk_r)
                ssrc = sst2[:, st:st + 1]
            S.op("act", lambda e: e.activation(out=lnp[:, st:st + 1], in_=ssrc, func=AF.Ln, scale=1.0 / D, bias=eps_t[:]),
                 reads=[rpost_r[st], eps_r], writes=[rpost_r[st]])
            S.op("act", lambda e: e.activation(out=rpost[:, st:st + 1], in_=lnp[:, st:st + 1], func=AF.Exp, scale=-0.5),
                 reads=[rpost_r[st]], writes=[rpost_r[st]])

        def do_post(st):
            gi, fused = post
            if fused:
                S.op("dve", lambda e: e.scalar_tensor_tensor(
                    out=xs[:, st, :], in0=ybv_[:, st, :], scalar=rpost[:, st:st + 1], in1=xs[:, st, :],
                    op0=ALU.mult, op1=ALU.add), reads=[yb_r[st], rpost_r[st], xs_r[st]], writes=[xs_r[st]])
            else:
                S.op("dve", lambda e: e.scalar_tensor_tensor(
                    out=ybv_[:, st, :], in0=ybv_[:, st, :], scalar=rpost[:, st:st + 1], in1=gbuf[gi][:],
                    op0=ALU.mult, op1=ALU.mult), reads=[yb_r[st], rpost_r[st], gb_r[gi]], writes=[yb_r[st]])
                S.op("dve", lambda e: e.tensor_tensor(out=xs[:, st, :], in0=xs[:, st, :], in1=ybv_[:, st, :], op=ALU.add),
                     reads=[yb_r[st], xs_r[st]], writes=[xs_r[st]])

        def pre_a(st):
            S.op("act", lambda e: e.activation(out=junk, in_=src[:, st, :], func=AF.Square,
                                               accum_out=ss[:, st, 0:1]), reads=[src_r[st]], writes=[spre_r[st]] + junk_r)
            S.op("act", lambda e: e.activation(out=lnv[:, st:st + 1], in_=ss[:, st, 0:1], func=AF.Ln, scale=1.0 / D, bias=eps_t[:]),
                 reads=[spre_r[st], eps_r], writes=[rpre_r[st]])
            S.op("act", lambda e: e.activation(out=rstd[:, st:st + 1], in_=lnv[:, st:st + 1], func=AF.Exp, scale=-0.5),
                 reads=[rpre_r[st]], writes=[rpre_r[st]])

        def pre_b(st):
            xi = st % 2
            S.op("dve", lambda e: e.scalar_tensor_tensor(
                out=xn[xi][:], in0=src[:, st, :], scalar=rstd[:, st:st + 1], in1=gbuf[gpre][:],
                op0=ALU.mult, op1=ALU.mult), reads=[src_r[st], rpre_r[st], gb_r[gpre]], writes=[xn_r[xi]])
            for half in range(2):
                b = next_bank()
                pv = psbf(b)
                for j in range(8):
                    kc = half * 8 + j
                    S.op("pe", lambda e, pv=pv, j=j, kc=kc: e.transpose(
                        out=pv[:, j * 128:(j + 1) * 128], in_=xn[xi][:, kc * 128:(kc + 1) * 128], identity=ident[:]),
                        reads=[xn_r[xi], c_r], writes=[ps_r[b]], selfsync=False)
                src_v = pv.rearrange("p (a b) -> p a b", a=8)
                dst_v = dst[:, half * 8:(half + 1) * 8, st * 128:(st + 1) * 128]
                if half == 0:
                    S.op("act", lambda e, s_=src_v, d=dst_v: e.copy(out=d, in_=s_), reads=[ps_r[b]], writes=[dst_r[st]])
                else:
                    S.op("dve", lambda e, s_=src_v, d=dst_v: e.tensor_copy(out=d, in_=s_), reads=[ps_r[b]], writes=[dst_r[st]])
                rel_bank(b)

        if post is not None:
            for st in range(NST):
                post_stats(st)
        for step in range(NST + 1):
            if step < NST:
                if post is not None:
                    do_post(step)
                if pre is not None:
                    pre_a(step)
            if step >= 1 and pre is not None:
                pre_b(step - 1)

    def load_w(w_ap2d, ncols):
        i = next_wb()
        dma("pool", wbuf[i][:, :, 0:ncols], w_ap2d.rearrange("(k p) n -> p k n", p=128), [], [wb_r[i]], wb_s[i])
        return i

    def proj_fm(w_l, col0, ncols, consumer, src=hT, src_r=hT_r, ntok=T):
        wi = load_w(w_l[:, col0:col0 + ncols], ncols)
        for oc in range(ncols // 128):
            b = next_bank()
            for k in range(KC):
                S.op("pe", lambda e, b=b, k=k, oc=oc, wi=wi: e.matmul(
                    psum[b][:, 0:ntok], lhsT=wbuf[wi][:, k, oc * 128:(oc + 1) * 128], rhs=src[:, k, 0:ntok],
                    start=(k == 0), stop=(k == KC - 1)),
                    reads=[wb_r[wi]] + src_r, writes=[ps_r[b]], selfsync=False)
            consumer(oc, b)
            rel_bank(b)

    def proj_tm(w_l, nK, act, act_r, mode="fused", gi=None):
        ybv_ = yb[:].rearrange("p (s d) -> p s d", s=NST)
        for cq in range(4):
            banks = [next_bank() for _ in range(NST)]
            for kg in range(nK // KC):
                wi = load_w(w_l[kg * D:(kg + 1) * D, cq * 512:(cq + 1) * 512], 512)
                for st in range(NST):
                    for kk in range(KC):
                        k = kg * KC + kk
                        S.op("pe", lambda e, b=banks[st], k=k, kk=kk, st=st, wi=wi: e.matmul(
                            psum[b][:], lhsT=act[:, k, st * 128:(st + 1) * 128], rhs=wbuf[wi][:, kk, :],
                            start=(k == 0), stop=(k == nK - 1)),
                            reads=[wb_r[wi]] + act_r[st], writes=[ps_r[banks[st]]], selfsync=False)
            for st in range(NST):
                b = banks[st]
                dst = ybv_[:, st, cq * 512:(cq + 1) * 512]
                if mode == "fused":
                    S.op("dve", lambda e, b=b, d=dst, cq=cq: e.tensor_tensor(
                        out=d, in0=psum[b][:], in1=gbuf[gi][:, cq * 512:(cq + 1) * 512], op=ALU.mult),
                        reads=[ps_r[b], gb_r[gi]], writes=[yb_r[st]])
                    S.op("act", lambda e, b=b, st=st, cq=cq: e.activation(
                        out=junk[:, st * 512:(st + 1) * 512], in_=psum[b][:], func=AF.Square, accum_out=ssq[:, st, cq:cq + 1]),
                        reads=[ps_r[b], yb_r[st]], writes=[ssq_r[st], junk_r[st]])
                elif mode == "accum":
                    S.op("dve", lambda e, b=b, d=dst: e.tensor_tensor(out=d, in0=d, in1=psum[b][:], op=ALU.add),
                         reads=[ps_r[b], yb_r[st]], writes=[yb_r[st]])
                else:
                    S.op("dve", lambda e, b=b, d=dst: e.tensor_copy(out=d, in_=psum[b][:]),
                         reads=[ps_r[b]], writes=[yb_r[st]])
                rel_bank(b)

    ocT = Bb[:, 0:KC, :]
    misc_ap = Bb[:, KC:32, :].rearrange("p a b -> p (a b)").bitcast(F32)

    bm_r = res("Bmisc")
    sst_r = res("Sst")
    tails_r = res("tails")
    ybv = yb[:].rearrange("p (s d) -> p s d", s=NST)
    SCALE = 1.0 / float(np.sqrt(512.0))

    def evac_copy(i, out, in_, reads, writes):
        if i % 2 == 0:
            S.op("act", lambda e: e.copy(out=out, in_=in_), reads=reads, writes=writes)
        else:
            S.op("dve", lambda e: e.tensor_copy(out=out, in_=in_), reads=reads, writes=writes)

    def prologue():
        ntm = n_seq * NMEM
        nsm = ntm // 128
        dma("sp", xs[:, 0:nsm, :], mem_d.rearrange("(s p) d -> p s d", p=128), [], xs_r, x_ld)
        if nsm < NST:
            S.op("dve", lambda e: e.memset(xs[:, nsm:NST, :], 1.0), reads=[], writes=xs_r)
        vtmp = Bb[:, KC:32, :].rearrange("p a b -> p (a b)").rearrange("p (s d) -> p s d", s=NST)
        for l in range(L):
            boundary(None, ("g_mem", l))
            for u in range(4):
                def cons(oc, b, u=u):
                    kc = u * 4 + oc
                    evac_copy(oc, ocT[:, kc, :], psum[b][:], [ps_r[b]], oc_r)
                proj_fm(w_k_d[l], u * 512, 512, cons)
            dma("sp", kT_scr[l], ocT[:, :, 0:ntm], oc_r, [res("kT_scr%d" % l)], scr_s)
            proj_tm(w_v_d[l], KC, hT, [[r] for r in hT_r], mode="plain")
            for st in range(nsm):
                S.op("act", lambda e, st=st: e.copy(out=vtmp[:, st, :], in_=ybv[:, st, :]),
                     reads=[yb_r[st]], writes=[bm_r])
            dma("sp", v_scr[l], vtmp[:, 0:nsm, :], [bm_r], [res("v_scr%d" % l)], vscr_s)
        for l in range(L):
            res("kT_scr%d" % l).lw = (scr_s.sem, scr_s.count, None)
            res("v_scr%d" % l).lw = (vscr_s.sem, vscr_s.count, None)

    def mlp(l):
        act_r = [[oc_r[st], bm_r] for st in range(NST)]
        for half in range(2):
            for u in range(8):
                def cons(oc, b, u=u):
                    fc = u * 4 + oc
                    ri = fc % 2
                    S.op("dve", lambda e: e.tensor_scalar(out=rt[ri], in0=psum[b][:], scalar1=0.0, scalar2=None,
                                                          op0=ALU.max), reads=[ps_r[b]], writes=[rt_r[ri]])
                    S.op("act", lambda e: e.activation(out=Bb[:, fc, :], in_=rt[ri], func=AF.Square),
                         reads=[rt_r[ri]], writes=oc_r + [bm_r])
                proj_fm(w_up_d[l], half * 4096 + u * 512, 512, cons)
            proj_tm(w_dn_d[l][half * 4096:(half + 1) * 4096, :], 32, Bb, act_r, mode=("first" if half == 0 else "accum"))
        return (load_gb("g_mlp_post", l), False)

    def attn(l, seq):
        A1 = Arena(yb[:], 32768, "at1")
        kTs, kTs_r = A1.alloc("kT", [KC, NMEM], BF16)
        Vs, Vs_r = A1.alloc("V", [2, D], BF16)
        qT, qT_r = A1.alloc("qT", [KC, T], BF16)
        A2 = Arena(misc_ap, 16384, "at2")
        ex, ex_r = A2.alloc("ex", [4, NMEM], F32)
        Pn, Pn_r = A2.alloc("Pn", [4, NMEM], BF16)
        PT, PT_r = A2.alloc("PT", [4, 2, T], BF16)
        S.handoff(yb_r, A1.items)
        S.handoff([bm_r], A2.items)
        dma("sp", kTs, kT_scr[l][:, :, seq * NMEM:(seq + 1) * NMEM], [res("kT_scr%d" % l)], [kTs_r], kv_s)
        dma("sp", Vs, v_scr[l][:, 2 * seq:2 * seq + 2, :], [res("v_scr%d" % l)], [Vs_r], vv_s)
        for u in range(4):
            def cons(oc, b, u=u):
                evac_copy(oc, qT[:, u * 4 + oc, :], psum[b][:], [ps_r[b]], [qT_r])
            proj_fm(w_q_d[l], u * 512, 512, cons)
        st_r = res("attn_stats")
        def emit_scores(st):
            banks = [next_bank(), next_bank()]
            for head in range(4):
                b = banks[head // 2]
                for dc in range(4):
                    c = head * 4 + dc
                    S.op("pe", lambda e, b=b, c=c, head=head, dc=dc, st=st: e.matmul(
                        psum[b][:, (head % 2) * NMEM:(head % 2 + 1) * NMEM], lhsT=qT[:, c, st * 128:(st + 1) * 128],
                        rhs=kTs[:, c, :], start=(dc == 0), stop=(dc == 3)),
                        reads=[qT_r, kTs_r], writes=[ps_r[b]], selfsync=False)
            return banks

        sbanks = {0: emit_scores(0), 1: emit_scores(1)}
        for st in range(NST):
            if st + 2 < NST:
                sbanks[st + 2] = emit_scores(st + 2)
            banks = sbanks[st]
            for i in range(2):
                b = banks[i]
                S.op("dve", lambda e, b=b, i=i: e.tensor_reduce(
                    out=amax[:, 2 * i:2 * i + 2], in_=psum[b][:].rearrange("p (a n) -> p a n", a=2), axis=AX.X, op=ALU.max),
                    reads=[ps_r[b]], writes=[st_r])
            S.op("dve", lambda e: e.tensor_scalar(out=anm[:], in0=amax[:], scalar1=-SCALE, scalar2=None, op0=ALU.mult),
                 reads=[st_r], writes=[st_r])
            for head in range(4):
                b = banks[head // 2]
                S.op("act", lambda e, b=b, head=head: e.activation(
                    out=ex[:, head, :], in_=psum[b][:, (head % 2) * NMEM:(head % 2 + 1) * NMEM], func=AF.Exp,
                    scale=SCALE, bias=anm[:, head:head + 1], accum_out=asum[:, head:head + 1]),
                    reads=[ps_r[b], st_r], writes=[ex_r, st_r])
            S.op("dve", lambda e: e.reciprocal(out=arinv[:], in_=asum[:]), reads=[st_r], writes=[st_r])
            rel_bank(banks[0])
            rel_bank(banks[1])
            for head in range(4):
                S.op("dve", lambda e, head=head: e.tensor_scalar(
                    out=Pn[:, head, :], in0=ex[:, head, :], scalar1=arinv[:, head:head + 1], scalar2=None, op0=ALU.mult),
                    reads=[ex_r, st_r], writes=[Pn_r])
            bt = next_bank()
            pv = psbf(bt)
            for head in range(4):
                for ncn in range(2):
                    j = head * 2 + ncn
                    S.op("pe", lambda e, pv=pv, j=j, head=head, ncn=ncn: e.transpose(
                        out=pv[:, j * 128:(j + 1) * 128], in_=Pn[:, head, ncn * 128:(ncn + 1) * 128], identity=ident[:]),
                        reads=[Pn_r, c_r], writes=[ps_r[bt]], selfsync=False)
            S.op("act", lambda e, pv=pv, st=st: e.copy(
                out=PT[:, :, :, st * 128:(st + 1) * 128], in_=pv.rearrange("p (h c t) -> p h c t", h=4, c=2)),
                reads=[ps_r[bt]], writes=[PT_r])
            rel_bank(bt)
        for head in range(4):
            for dc in range(4):
                c = head * 4 + dc
                b = next_bank()
                for ncn in range(2):
                    S.op("pe", lambda e, b=b, c=c, head=head, ncn=ncn: e.matmul(
                        psum[b][:], lhsT=Vs[:, ncn, c * 128:(c + 1) * 128], rhs=PT[:, head, ncn, :],
                        start=(ncn == 0), stop=(ncn == 1)),
                        reads=[Vs_r, PT_r], writes=[ps_r[b]], selfsync=False)
                evac_copy(c, ocT[:, c, :], psum[b][:], [ps_r[b]], oc_r)
                rel_bank(b)
        S.handoff(A1.items, yb_r)
        S.handoff(A2.items, [bm_r])
        gi = load_gb("g_x_post", l)
        proj_tm(w_xo_d[l], KC, ocT, [[r] for r in oc_r], mode="fused", gi=gi)
        return (gi, True)

    def mixer(l, first_in_seq):
        if first_in_seq:
            S.op("dve", lambda e: e.memset(Sst[:], 0.0), reads=[], writes=[sst_r])
            S.op("dve", lambda e: e.memset(tails[:, l, :, :], 0.0), reads=[], writes=[tails_r])
        else:
            dma("sp", Sst[:].rearrange("p a b -> p (a b)"), s_scr[l], [res("s_scr%d" % l)], [sst_r], sst_s)
        A1 = Arena(yb[:], 32768, "mx1")
        A2 = Arena(misc_ap, 16384, "mx2")
        sgf = [A1.alloc("sgf%d" % i, [T], F32) for i in range(2)]
        qs = [A1.alloc("qs%d" % i, [T], F32) for i in range(2)]
        vsb = [A1.alloc("vsb%d" % i, [NST, 128], BF16) for i in range(2)]
        sgt = [A1.alloc("sgt%d" % i, [T], F32) for i in range(2)]
        X1, X1_r = A1.alloc("X1", [T], F32)
        X2, X2_r = A1.alloc("X2", [T], F32)
        X3, X3_r = A1.alloc("X3", [T], F32)
        X45, X45_r = A1.alloc("X45", [2 * T], F32)
        X4, X5 = X45[:, 0:T], X45[:, T:2 * T]
        X4_r = X5_r = X45_r
        Q64, Q64_r = A1.alloc("Q64", [T], BF16)
        KH, KH_r = A1.alloc("KH", [T], BF16)
        Q0, Q0_r = A1.alloc("Q0", [T], BF16)
        K0, K0_r = A1.alloc("K0", [T], BF16)
        khTlo, khTlo_r = A1.alloc("khTlo", [NST, 128], BF16)
        khThi, khThi_r = A1.alloc("khThi", [NST, 128], BF16)
        Sall, Sall_r = A2.alloc("Sall", [7, 128], F32)
        Sbf, Sbf_r = A2.alloc("Sbf", [8, 128], BF16)
        sq, sq_r = A2.alloc("sq", [T], F32)
        rb, rb_r = A2.alloc("rb", [T], F32)
        t1, t1_r = A2.alloc("t1", [T], F32)
        scm, scm_r = A2.alloc("scm", [NST, 128], BF16)
        K16, K16_r = A2.alloc("K16", [T], BF16)
        Q2, Q2_r = A2.alloc("Q2", [T], BF16)
        K32, K32_r = A2.alloc("K32", [T], BF16)
        lf, lf_r = X1, X1_r
        bcs, bcs_r = X3, X3_r
        km, km_r = X2, X2_r
        ucv, ucv_r = X45[:, 0:T + 2], X45_r
        qt, qt_r = Q64, Q64_r
        S.handoff(yb_r, A1.items)
        S.handoff([bm_r], A2.items)
        S.op("dve", lambda e: e.memset(khTlo[64:128, :, :], 0.0), reads=[], writes=[khTlo_r])
        S.op("dve", lambda e: e.memset(khThi[0:64, :, :], 0.0), reads=[], writes=[khThi_r])
        el_r = res("elast")
        wl = w_in_d[l]

        def P(h):
            wi = load_w(wl[:, h * 512:(h + 1) * 512], 512)
            bk = {}
            for name, oc in (("q", 0), ("f", 1), ("g", 3)):
                b = next_bank()
                bk[name] = b
                for k in range(KC):
                    S.op("pe", lambda e, b=b, k=k, oc=oc, wi=wi: e.matmul(
                        psum[b][:], lhsT=wbuf[wi][:, k, oc * 128:(oc + 1) * 128], rhs=hT[:, k, :],
                        start=(k == 0), stop=(k == KC - 1)), reads=[wb_r[wi]] + hT_r, writes=[ps_r[b]], selfsync=False)
            b = next_bank()
            bk["v"] = b
            for st in range(NST):
                for k in range(KC):
                    S.op("pe", lambda e, b=b, k=k, st=st, wi=wi: e.matmul(
                        psum[b][:, st * 128:(st + 1) * 128], lhsT=hT[:, k, st * 128:(st + 1) * 128],
                        rhs=wbuf[wi][:, k, 256:384], start=(k == 0), stop=(k == KC - 1)),
                        reads=[wb_r[wi], hT_r[st]], writes=[ps_r[b]], selfsync=False)
            return bk

        def E(h, bk):
            p = h % 2
            S.op("act", lambda e: e.activation(out=sgf[p][0], in_=psum[bk["f"]][:], func=AF.Sigmoid),
                 reads=[ps_r[bk["f"]]], writes=[sgf[p][1]])
            S.op("act", lambda e: e.activation(out=qs[p][0], in_=psum[bk["q"]][:], func=AF.Silu),
                 reads=[ps_r[bk["q"]]], writes=[qs[p][1]])
            S.op("act", lambda e: e.activation(out=sgt[p][0], in_=psum[bk["g"]][:], func=AF.Silu),
                 reads=[ps_r[bk["g"]]], writes=[sgt[p][1]])
            S.op("dve", lambda e: e.tensor_copy(out=vsb[p][0], in_=psum[bk["v"]][:].rearrange("p (s v) -> p s v", s=NST)),
                 reads=[ps_r[bk["v"]]], writes=[vsb[p][1]])
            for b in bk.values():
                rel_bank(b)

        def Rh(h):
            p = h % 2
            f, f_r = sgf[p]
            qsp, qsp_r = qs[p]
            v3 = lambda ap, c: ap.rearrange("p (c t) -> p c t", c=c)
            S.op("dve", lambda e: e.tensor_scalar(out=f, in0=f, scalar1=oml[:, l, h:h + 1], scalar2=lb[:, l, h:h + 1],
                                                  op0=ALU.mult, op1=ALU.add), reads=[f_r, lb_r], writes=[f_r])
            S.op("act", lambda e: e.activation(out=X1, in_=f, func=AF.Ln), reads=[f_r], writes=[X1_r])
            S.op("dve", lambda e: e.tensor_scalar(out=X2, in0=f, scalar1=-1.0, scalar2=1.0, op0=ALU.mult, op1=ALU.add),
                 reads=[f_r], writes=[X2_r])
            for dst, dst_r, mi in ((X3, X3_r, 0), (X4, X4_r, 1), (X5, X5_r, 2)):
                S.op("dve", lambda e, dst=dst, mi=mi: e.tensor_tensor_scan(out=dst, data0=masks[:, mi, :], data1=X1, initial=0.0,
                                                                           op0=ALU.mult, op1=ALU.add),
                     reads=[X1_r, c_r], writes=[dst_r])
            S.op("dve", lambda e: e.tensor_scalar(out=X4, in0=X4, scalar1=-80.0, scalar2=None, op0=ALU.max),
                 reads=[X4_r], writes=[X4_r])
            S.op("act", lambda e: e.activation(out=X1, in_=X3, func=AF.Exp), reads=[X3_r], writes=[X1_r])
            S.op("dve", lambda e: e.tensor_copy(out=elast[:], in_=v3(X1, 8)[:, :, 63]), reads=[X1_r], writes=[el_r])
            S.op("dve", lambda e: e.tensor_tensor(out=Q64, in0=qsp, in1=X1, op=ALU.mult), reads=[qsp_r, X1_r], writes=[Q64_r])
            S.op("dve", lambda e: e.tensor_copy(out=bl64[:], in_=v3(X3, 8)[:, :, 63]), reads=[X3_r], writes=[el_r])
            S.op("dve", lambda e: e.tensor_tensor(out=v3(X3, 8), in0=v3(X3, 8),
                                                  in1=bl64[:].unsqueeze(2).broadcast_to([128, 8, 64]), op=ALU.subtract),
                 reads=[X3_r, el_r], writes=[X3_r])
            S.op("act", lambda e: e.activation(out=X3, in_=X3, func=AF.Exp, scale=-1.0), reads=[X3_r], writes=[X3_r])
            S.op("dve", lambda e: e.tensor_tensor(out=KH, in0=X2, in1=X3, op=ALU.mult), reads=[X2_r, X3_r], writes=[KH_r])
            S.op("act", lambda e: e.activation(out=X1, in_=X4, func=AF.Exp), reads=[X4_r], writes=[X1_r])
            S.op("dve", lambda e: e.tensor_copy(out=el16[:], in_=v3(X1, 32)[:, :, 15]), reads=[X1_r], writes=[el_r])
            S.op("dve", lambda e: e.tensor_tensor(out=Q0, in0=qsp, in1=X1, op=ALU.mult), reads=[qsp_r, X1_r], writes=[Q0_r])
            S.op("act", lambda e: e.activation(out=X4, in_=X4, func=AF.Exp, scale=-1.0), reads=[X4_r], writes=[X4_r])
            S.op("dve", lambda e: e.tensor_tensor(out=X4, in0=X2, in1=X4, op=ALU.mult), reads=[X2_r, X4_r], writes=[X4_r])
            S.op("act", lambda e: e.copy(out=K0, in_=X4), reads=[X4_r], writes=[K0_r])
            S.op("dve", lambda e: e.tensor_tensor(out=v3(K16, 32), in0=v3(X4, 32),
                                                  in1=el16[:].unsqueeze(2).broadcast_to([128, 32, 16]), op=ALU.mult),
                 reads=[X4_r, el_r], writes=[K16_r])
            S.op("act", lambda e: e.activation(out=X1, in_=X5, func=AF.Exp), reads=[X5_r], writes=[X1_r])
            S.op("dve", lambda e: e.tensor_tensor(out=Q2, in0=qsp, in1=X1, op=ALU.mult), reads=[qsp_r, X1_r], writes=[Q2_r])
            S.op("dve", lambda e: e.tensor_copy(out=bl32[:], in_=v3(X5, 16)[:, :, 31]), reads=[X5_r], writes=[el_r])
            S.op("dve", lambda e: e.tensor_tensor(out=v3(X5, 16), in0=v3(X5, 16),
                                                  in1=bl32[:].unsqueeze(2).broadcast_to([128, 16, 32]), op=ALU.subtract),
                 reads=[X5_r, el_r], writes=[X5_r])
            S.op("act", lambda e: e.activation(out=X5, in_=X5, func=AF.Exp, scale=-1.0), reads=[X5_r], writes=[X5_r])
            S.op("dve", lambda e: e.tensor_tensor(out=K32, in0=X2, in1=X5, op=ALU.mult), reads=[X2_r, X5_r], writes=[K32_r])
            bT = next_bank()
            pv = psbf(bT)
            for st in range(NST):
                S.op("pe", lambda e, st=st: e.transpose(out=pv[:, st * 128:(st + 1) * 128], in_=KH[:, st * 128:(st + 1) * 128],
                                                        identity=ident[:]), reads=[KH_r, c_r], writes=[ps_r[bT]], selfsync=False)
            S.op("act", lambda e: e.copy(out=khTlo[0:64, :, :], in_=pv[0:64, 0:T].rearrange("p (s k) -> p s k", s=NST)),
                 reads=[ps_r[bT]], writes=[khTlo_r])
            S.op("act", lambda e: e.copy(out=khThi[64:128, :, :], in_=pv[64:128, 0:T].rearrange("p (s k) -> p s k", s=NST)),
                 reads=[ps_r[bT]], writes=[khThi_r])
            rel_bank(bT)
            bP = []
            for (kk, kk_r, qq, qq_r) in ((K0, K0_r, Q0, Q0_r), (K16, K16_r, Q0, Q0_r), (K32, K32_r, Q2, Q2_r)):
                b = next_bank()
                bP.append(b)
                for st in range(NST):
                    S.op("pe", lambda e, st=st, b=b, kk=kk, qq=qq: e.matmul(
                        psum[b][:, st * 128:(st + 1) * 128], lhsT=kk[:, st * 128:(st + 1) * 128],
                        rhs=qq[:, st * 128:(st + 1) * 128], start=True, stop=True),
                        reads=[kk_r, qq_r], writes=[ps_r[b]], selfsync=False)
            S.op("dve", lambda e: e.tensor_tensor(out=sq, in0=psum[bP[0]][:], in1=masks[:, 3, :], op=ALU.mult),
                 reads=[ps_r[bP[0]], c_r], writes=[sq_r])
            S.op("dve", lambda e: e.tensor_tensor(out=t1, in0=psum[bP[1]][:], in1=masks[:, 4, :], op=ALU.mult),
                 reads=[ps_r[bP[1]], c_r], writes=[t1_r])
            S.op("dve", lambda e: e.tensor_tensor(out=sq, in0=sq, in1=t1, op=ALU.add), reads=[sq_r, t1_r], writes=[sq_r])
            S.op("dve", lambda e: e.tensor_tensor(out=t1, in0=psum[bP[2]][:], in1=masks[:, 5, :], op=ALU.mult),
                 reads=[ps_r[bP[2]], c_r], writes=[t1_r])
            S.op("dve", lambda e: e.tensor_tensor(out=scm.rearrange("p s t -> p (s t)"), in0=sq, in1=t1, op=ALU.add),
                 reads=[sq_r, t1_r], writes=[scm_r])
            for b in bP:
                rel_bank(b)
            bU = [next_bank(), next_bank()]
            for c in range(8):
                st, hf = c // 2, c % 2
                S.op("pe", lambda e, c=c, st=st, hf=hf: e.matmul(
                    psum[bU[c // 4]][:, (c % 4) * 128:(c % 4 + 1) * 128], lhsT=(khTlo if hf == 0 else khThi)[:, st, :],
                    rhs=vsb[p][0][:, st, :], start=True, stop=True),
                    reads=[khTlo_r, khThi_r, vsb[p][1]], writes=[ps_r[bU[c // 4]]], selfsync=False)
            S.op("act", lambda e: e.copy(out=Sbf[:, 0, :], in_=Sst[:, h, :]), reads=[sst_r], writes=[Sbf_r])
            for c in range(8):
                cur = Sst[:, h, :] if c == 0 else Sall[:, c - 1, :]
                out = Sst[:, h, :] if c == 7 else Sall[:, c, :]
                rd = [ps_r[bU[c // 4]], el_r, sst_r if c == 0 else Sall_r]
                wr = [sst_r] if c == 7 else [Sall_r]
                S.op("dve", lambda e, c=c, cur=cur, out=out: e.scalar_tensor_tensor(
                    out=out, in0=cur, scalar=elast[:, c:c + 1], in1=psum[bU[c // 4]][:, (c % 4) * 128:(c % 4 + 1) * 128],
                    op0=ALU.mult, op1=ALU.add), reads=rd, writes=wr)
            S.op("act", lambda e: e.copy(out=Sbf[:, 1:8, :], in_=Sall[:, 0:7, :]), reads=[Sall_r], writes=[Sbf_r])
            rel_bank(bU[0])
            rel_bank(bU[1])
            bO = next_bank()
            for st in range(NST):
                S.op("pe", lambda e, st=st: e.matmul(psum[bO][:, st * 128:(st + 1) * 128], lhsT=vsb[p][0][:, st, :],
                                                     rhs=scm[:, st, :], start=True, stop=False),
                     reads=[vsb[p][1], scm_r], writes=[ps_r[bO]], selfsync=False)
                for c in (2 * st, 2 * st + 1):
                    S.op("pe", lambda e, c=c: e.matmul(psum[bO][:, c * 64:(c + 1) * 64], lhsT=Sbf[:, c, :],
                                                       rhs=qt[:, c * 64:(c + 1) * 64], start=False, stop=(c % 2 == 1)),
                         reads=[Sbf_r, qt_r], writes=[ps_r[bO]], selfsync=False)
            groupnorm_out(bO, None, hgs[:, l, h:h + 1], sgt[p], h)
            rel_bank(bO)

        def groupnorm_out(bO, z_sb, gcol, gate, chunk):
            if z_sb is None:
                src, src_r = psum[bO][:], ps_r[bO]
            else:
                src, src_r = z_sb
            S.op("act", lambda e: e.activation(out=sq, in_=src, func=AF.Square), reads=[src_r], writes=[sq_r])
            bN = next_bank()
            S.op("pe", lambda e: e.matmul(psum[bN][:], lhsT=ones[:], rhs=sq, start=True, stop=True),
                 reads=[sq_r, ones_r], writes=[ps_r[bN]], selfsync=False)
            S.op("act", lambda e: e.activation(out=rb, in_=psum[bN][:], func=AF.Ln, scale=1.0 / 128, bias=eps_t[:]),
                 reads=[ps_r[bN], eps_r], writes=[rb_r])
            S.op("act", lambda e: e.activation(out=rb, in_=rb, func=AF.Exp, scale=-0.5), reads=[rb_r], writes=[rb_r])
            rel_bank(bN)
            if gate is None:
                S.op("dve", lambda e: e.scalar_tensor_tensor(out=ocT[:, chunk, :], in0=src, scalar=gcol, in1=rb,
                                                             op0=ALU.mult, op1=ALU.mult),
                     reads=[src_r, rb_r, c_r], writes=oc_r)
            else:
                S.op("dve", lambda e: e.tensor_tensor(out=t1, in0=src, in1=rb, op=ALU.mult), reads=[src_r, rb_r], writes=[t1_r])
                S.op("dve", lambda e: e.scalar_tensor_tensor(out=ocT[:, chunk, :], in0=t1, scalar=gcol, in1=gate[0],
                                                             op0=ALU.mult, op1=ALU.mult),
                     reads=[t1_r, gate[1], c_r], writes=oc_r)

        def Pc(j):
            wi = load_w(wl[:, 4096 + j * 384:4096 + (j + 1) * 384], 384)
            bk = []
            for oc in range(3):
                b = next_bank()
                bk.append(b)
                for k in range(KC):
                    S.op("pe", lambda e, b=b, k=k, oc=oc, wi=wi: e.matmul(
                        psum[b][:], lhsT=wbuf[wi][:, k, oc * 128:(oc + 1) * 128], rhs=hT[:, k, :],
                        start=(k == 0), stop=(k == KC - 1)), reads=[wb_r[wi]] + hT_r, writes=[ps_r[b]], selfsync=False)
            return bk

        def Rc(j, bk):
            bB, bC, bH = bk
            ccs, ccs_r = lf, lf_r
            ycv, ycv_r = bcs, bcs_r
            z, z_r = km, km_r
            S.op("act", lambda e: e.copy(out=ccs, in_=psum[bC][:]), reads=[ps_r[bC]], writes=[ccs_r])
            S.op("act", lambda e: e.copy(out=ucv[:, 0:2], in_=tails[:, l, j, :]), reads=[tails_r], writes=[ucv_r])
            S.op("dve", lambda e: e.tensor_tensor(out=ucv[:, 2:T + 2], in0=ccs, in1=psum[bH][:], op=ALU.mult),
                 reads=[ccs_r, ps_r[bH], ucv_r], writes=[ucv_r])
            S.op("act", lambda e: e.copy(out=tails[:, l, j, :], in_=ucv[:, T:T + 2]), reads=[ucv_r], writes=[tails_r])
            S.op("dve", lambda e: e.tensor_scalar(out=ycv, in0=ucv[:, 0:T], scalar1=cws[:, l, 0, j:j + 1], scalar2=None,
                                                  op0=ALU.mult), reads=[ucv_r, c_r], writes=[ycv_r])
            for tap in (1, 2):
                S.op("dve", lambda e, tap=tap: e.scalar_tensor_tensor(
                    out=ycv, in0=ucv[:, tap:T + tap], scalar=cws[:, l, tap, j:j + 1], in1=ycv, op0=ALU.mult, op1=ALU.add),
                    reads=[ucv_r, ycv_r, c_r], writes=[ycv_r])
            S.op("dve", lambda e: e.tensor_tensor(out=z, in0=ycv, in1=psum[bB][:], op=ALU.mult),
                 reads=[ycv_r, ps_r[bB]], writes=[z_r])
            for b in bk:
                rel_bank(b)
            groupnorm_out(None, (z, z_r), cgs[:, l, j:j + 1], None, 8 + j)

        bk = P(0)
        E(0, bk)
        for h in range(8):
            if h + 1 < 8:
                bk = P(h + 1)
            else:
                bkc = Pc(0)
            Rh(h)
            if h + 1 < 8:
                E(h + 1, bk)
        for j in range(8):
            cur = bkc
            if j + 1 < 8:
                bkc = Pc(j + 1)
            Rc(j, cur)
        dma("sp", s_scr[l], Sst[:].rearrange("p a b -> p (a b)"), [sst_r], [res("s_scr%d" % l)], sst_s)
        S.handoff(A1.items, yb_r)
        S.handoff(A2.items, [bm_r])
        gi = load_gb("g_mix_post", l)
        proj_tm(w_mo_d[l], KC, ocT, [[r] for r in oc_r], mode="fused", gi=gi)
        return (gi, True)

    if "attn" in phases:
        prologue()
    for tile in range(n_tiles):
        seq = tile // tiles_per_seq
        first_in_seq = (tile % tiles_per_seq == 0)
        dma("sp", xs[:], x_d[tile * T:(tile + 1) * T, :].rearrange("(s p) d -> p s d", p=128), [res("out")], xs_r, x_ld)
        seqb = []
        for l in range(L):
            if "mix" in phases:
                seqb.append(("mix", l, "g_mix_pre"))
            if "attn" in phases:
                seqb.append(("attn", l, "g_x_pre"))
            if "mlp" in phases:
                seqb.append(("mlp", l, "g_mlp_pre"))
        boundary(None, (seqb[0][2], seqb[0][1]))
        for i, (kind, l, _) in enumerate(seqb):
            if kind == "mix":
                post = mixer(l, first_in_seq)
            elif kind == "attn":
                post = attn(l, seq)
            else:
                post = mlp(l)
            nxt = (seqb[i + 1][2], seqb[i + 1][1]) if i + 1 < len(seqb) else None
            boundary(post, nxt)
        dma("sp", out_d[tile * T:(tile + 1) * T, :].rearrange("(s p) d -> p s d", p=128), xs[:], xs_r, [res("out")], x_st)

    with nc.Block() as block:
        S.replay(block, [(x_st.sem, x_st.count)])
    es.close()
    build.stats = {e: len(S.ops[e]) for e in S.ENGS}
    build.stats["waits"] = S.nwaits
    return nc


def _host_consts():
    import ml_dtypes
    ident = np.eye(128, dtype=np.float32).astype(ml_dtypes.bfloat16)
    t = np.arange(T)
    masks = np.zeros((128, 6, T), np.float32)
    masks[:, 0, :] = (t % 64 != 0)[None, :]
    masks[:, 1, :] = (t % 16 != 0)[None, :]
    masks[:, 2, :] = (t % 32 != 0)[None, :]
    s_ = np.arange(128)[:, None]
    tt = np.arange(128)[None, :]
    m0 = (s_ // 16 == tt // 16) & (s_ <= tt)
    m1 = (s_ // 32 == tt // 32) & ((s_ // 16) % 2 == 0) & ((tt // 16) % 2 == 1)
    m2 = (s_ // 64 == tt // 64) & ((s_ // 32) % 2 == 0) & ((tt // 32) % 2 == 1)
    for i, m in enumerate((m0, m1, m2)):
        masks[:, 3 + i, :] = np.tile(m.astype(np.float32), (1, NST))
    return ident, masks.astype(ml_dtypes.bfloat16)


def _perm_w_in(w_in):
    L = w_in.shape[0]
    idx = []
    for h in range(8):
        for part in range(4):
            idx.extend(range(part * 1024 + h * 128, part * 1024 + (h + 1) * 128))
    for j in range(8):
        for part in range(3):
            idx.extend(range(4096 + part * 1024 + j * 128, 4096 + part * 1024 + (j + 1) * 128))
    idx = np.asarray(idx)
    return np.ascontiguousarray(w_in[:, :, idx])


def make_in_maps(inputs, n_cores, n_layers, n_tiles, tiles_per_seq, n_seq):
    f = lambda a: np.ascontiguousarray(np.asarray(a, dtype=np.float32))
    L = n_layers
    ident, masks = _host_consts()
    shared = {
        "w_in": _perm_w_in(f(inputs["w_in"])[:L]),
        "w_mix_out": f(inputs["w_mix_out"])[:L], "w_q": f(inputs["w_q"])[:L], "w_k": f(inputs["w_k"])[:L],
        "w_v": f(inputs["w_v"])[:L], "w_xo": f(inputs["w_xo"])[:L], "w_up": f(inputs["w_up"])[:L],
        "w_down": f(inputs["w_down"])[:L],
        "ident": ident, "masks": masks,
        "lbT": np.ascontiguousarray(f(inputs["hgrn_lb_logits"])[:L].reshape(L, 8, 128).transpose(2, 0, 1)),
        "hgT": np.ascontiguousarray(f(inputs["hgrn_norm_g"])[:L].reshape(L, 8, 128).transpose(2, 0, 1)),
        "cgT": np.ascontiguousarray(f(inputs["conv_norm_g"])[:L].reshape(L, 8, 128).transpose(2, 0, 1)),
        "cwT": np.ascontiguousarray(f(inputs["conv_w"])[:L].reshape(L, 3, 8, 128).transpose(3, 0, 1, 2)),
    }
    for n in ["g_mix_pre", "g_mix_post", "g_x_pre", "g_mem", "g_x_post", "g_mlp_pre", "g_mlp_post"]:
        shared[n] = f(inputs[n])[:L]
    x = f(inputs["x"])
    mem = f(inputs["mem"])
    seq_len = tiles_per_seq * T
    maps = []
    for c in range(n_cores):
        m = dict(shared)
        m["x"] = np.ascontiguousarray(x[c * n_seq:(c + 1) * n_seq, :seq_len].reshape(n_seq * seq_len, D))
        m["mem"] = np.ascontiguousarray(mem[c * n_seq:(c + 1) * n_seq].reshape(n_seq * NMEM, D))
        maps.append(m)
    return maps


_NC_CACHE = {}


def kernel(**inputs):
    n_cores, n_seq, tiles_per_seq = 8, 2, 4
    key = "full"
    if key not in _NC_CACHE:
        _NC_CACHE[key] = build(4, n_seq * tiles_per_seq, tiles_per_seq, ("mix", "attn", "mlp"), n_seq)
    nc = _NC_CACHE[key]
    maps = make_in_maps(inputs, n_cores, 4, n_seq * tiles_per_seq, tiles_per_seq, n_seq)
    r = run_bass_kernel_spmd(nc, maps, core_ids=list(range(n_cores)))
    outs = [np.asarray(r.results[c]["out"]).reshape(n_seq, tiles_per_seq * T, D) for c in range(n_cores)]
    return np.concatenate(outs, axis=0).astype(np.float32)
```

```python
import contextlib
import numpy as np
import concourse.bass as bass
import concourse.mybir as mybir
from concourse.bass_utils import run_bass_kernel_spmd

F32 = mybir.dt.float32
BF16 = mybir.dt.bfloat16
AF = mybir.ActivationFunctionType
ALU = mybir.AluOpType
AX = mybir.AxisListType

D = 2048
T = 512
NST = 4
KC = 16
NMEM = 256
DFF = 8192
EPS = 1e-6
IN_W = 7168


class Res:
    __slots__ = ("name", "lw", "rd")

    def __init__(self, name=""):
        self.name = name
        self.lw = None
        self.rd = []


class DSem:
    __slots__ = ("sem", "count")

    def __init__(self, sem):
        self.sem = sem
        self.count = 0


class Sched:
    ENGS = ("pe", "act", "dve", "pool", "sp")

    def __init__(self, sems):
        self.sem = sems
        self.ops = {e: [] for e in self.ENGS}
        self.cnt = {e: 0 for e in self.ENGS}
        self.waited = {e: {} for e in self.ENGS}
        self.nwaits = 0

    def op(self, eng, fn, reads=(), writes=(), dsem=None, selfsync=True):
        need = {}

        def add(tok):
            if tok is None:
                return
            sem, val, src = tok
            if src == eng and not selfsync:
                return
            k = sem.num
            if k not in need or need[k][1] < val:
                need[k] = (sem, val)

        for r in reads:
            add(r.lw)
        for w in writes:
            add(w.lw)
            for t in w.rd:
                add(t)
        waits = []
        wd = self.waited[eng]
        for k, (sem, val) in need.items():
            if wd.get(k, 0) >= val:
                continue
            wd[k] = val
            waits.append((sem, val))
        self.nwaits += len(waits)
        if dsem is not None:
            dsem.count += 16
            tok = (dsem.sem, dsem.count, None)
            inc = (dsem.sem, 16)
        else:
            self.cnt[eng] += 1
            tok = (self.sem[eng], self.cnt[eng], eng)
            inc = (self.sem[eng], 1)
        self.ops[eng].append((waits, fn, inc))
        for r in reads:
            r.rd.append(tok)
        for w in writes:
            w.lw = tok
            w.rd = []
        return tok

    def handoff(self, olds, news):
        toks = []
        for o in olds:
            if o.lw is not None:
                toks.append(o.lw)
            toks.extend(o.rd)
        best = {}
        for t in toks:
            k = t[0].num
            if k not in best or best[k][1] < t[1]:
                best[k] = t
        for n in news:
            n.rd.extend(best.values())

    def replay(self, block, final_waits):
        def run(name, e):
            for waits, fn, inc in self.ops[name]:
                for sem, val in waits:
                    e.wait_ge(sem, val)
                ins = fn(e)
                ins.then_inc(inc[0], inc[1])
            if name == "sp":
                for sem, val in final_waits:
                    e.wait_ge(sem, val)

        @block.tensor
        def _(e):
            run("pe", e)

        @block.scalar
        def _(e):
            run("act", e)

        @block.vector
        def _(e):
            run("dve", e)

        @block.gpsimd
        def _(e):
            run("pool", e)

        @block.sync
        def _(e):
            run("sp", e)


def build(n_layers=4, n_tiles=8, tiles_per_seq=4, phases=("mix", "attn", "mlp"), n_seq=2):
    L = n_layers
    nc = bass.Bass("TRN2", target_bir_lowering=False)
    es = contextlib.ExitStack()

    def dram(name, shape, dt, kind="ExternalInput"):
        return nc.dram_tensor(name, list(shape), dt, kind=kind).ap()

    x_d = dram("x", [n_tiles * T, D], F32)
    mem_d = dram("mem", [n_seq * NMEM, D], F32)
    out_d = dram("out", [n_tiles * T, D], F32, kind="ExternalOutput")
    w_in_d = dram("w_in", [L, D, IN_W], F32)
    w_mo_d = dram("w_mix_out", [L, D, D], F32)
    w_q_d = dram("w_q", [L, D, D], F32)
    w_k_d = dram("w_k", [L, D, D], F32)
    w_v_d = dram("w_v", [L, D, D], F32)
    w_xo_d = dram("w_xo", [L, D, D], F32)
    w_up_d = dram("w_up", [L, D, DFF], F32)
    w_dn_d = dram("w_down", [L, DFF, D], F32)
    gnames = ["g_mix_pre", "g_mix_post", "g_x_pre", "g_mem", "g_x_post", "g_mlp_pre", "g_mlp_post"]
    g_d = {n: dram(n, [L, D], F32) for n in gnames}
    lbl_d = dram("lbT", [128, L, 8], F32)
    hg_d = dram("hgT", [128, L, 8], F32)
    cg_d = dram("cgT", [128, L, 8], F32)
    cw_d = dram("cwT", [128, L, 3, 8], F32)
    ident_d = dram("ident", [128, 128], BF16)
    masks_d = dram("masks", [128, 6, T], BF16)
    kT_scr = dram("kT_scr", [L, 128, KC, n_seq * NMEM], BF16, kind="Internal")
    v_scr = dram("v_scr", [L, 128, 2 * n_seq, D], BF16, kind="Internal")
    s_scr = dram("s_scr", [L, 128, 8 * 128], F32, kind="Internal")

    def sb(name, shape, dt):
        return es.enter_context(nc.sbuf_tensor(name, list(shape), dt))

    def newsem(name):
        return es.enter_context(nc.semaphore(name))

    S = Sched({e: newsem("sem_" + e) for e in Sched.ENGS})

    xs = sb("xs", [128, NST, D], F32)
    yb = sb("yb", [128, NST * D], F32)
    hT = sb("hT", [128, KC, T], BF16)
    Bb = sb("Bb", [128, 32, T], BF16)
    NW = 3
    wbuf = [sb("wb%d" % i, [128, KC, 512], BF16) for i in range(NW)]
    gbuf = [sb("gb%d" % i, [128, D], F32) for i in range(2)]
    Sst = sb("Sst", [128, 8, 128], F32)
    xn = [sb("xn%d" % i, [128, D], BF16) for i in range(2)]
    rtb = sb("rtb", [128, 2 * T], F32)
    rt = [rtb[:, 0:T], rtb[:, T:2 * T]]
    junk = sb("junk", [128, D], BF16)[:]
    ssq = sb("ssq", [128, NST, 4], F32)
    sst2 = sb("sst2", [128, NST], F32)
    rpost = sb("rpost", [128, NST], F32)
    lnp = sb("lnp", [128, NST], F32)
    ident = sb("ident_s", [128, 128], BF16)
    ones = sb("ones_s", [128, 128], F32)
    masks = sb("masks_s", [128, 6, T], BF16)
    el16 = sb("el16", [128, 32], F32)
    bl64 = sb("bl64", [128, 8], F32)
    bl32 = sb("bl32", [128, 16], F32)
    lbt = sb("lbt", [128, L, 8], F32)
    lb = sb("lb", [128, L, 8], F32)
    oml = sb("oml", [128, L, 8], F32)
    lbe = sb("lbe", [128, L, 8], F32)
    lbm = sb("lbm", [128, 8], F32)
    hgs = sb("hgs", [128, L, 8], F32)
    cgs = sb("cgs", [128, L, 8], F32)
    cws = sb("cws", [128, L, 3, 8], F32)
    tails = sb("tails", [128, L, 8, 2], F32)
    ss = sb("ss", [128, NST, 4], F32)
    sstot = sb("sstot", [128, NST], F32)
    lnv = sb("lnv", [128, NST], F32)
    rstd = sb("rstd", [128, NST], F32)
    eps_t = sb("eps_t", [128, 1], F32)
    amax = sb("amax", [128, 4], F32)
    anm = sb("anm", [128, 4], F32)
    asum = sb("asum", [128, 4], F32)
    arinv = sb("arinv", [128, 4], F32)
    elast = sb("elast", [128, 8], F32)

    psum = [es.enter_context(nc.psum_tensor("ps%d" % i, [128, 512], F32)) for i in range(8)]

    R = {}

    def res(name):
        if name not in R:
            R[name] = Res(name)
        return R[name]

    rt_r = [res("rt0"), res("rt1")]
    xs_r = [res("xs%d" % i) for i in range(NST)]
    yb_r = [res("yb%d" % i) for i in range(NST)]
    hT_r = [res("hT%d" % i) for i in range(NST)]
    oc_r = [res("ocat%d" % i) for i in range(NST)]
    ps_r = [res("ps%d" % i) for i in range(8)]
    wb_r = [res("wb%d" % i) for i in range(NW)]
    wb_s = [DSem(newsem("wbs%d" % i)) for i in range(NW)]
    gb_r = [res("gb%d" % i) for i in range(2)]
    gb_s = [DSem(newsem("gbs%d" % i)) for i in range(2)]
    xn_r = [res("xn%d" % i) for i in range(2)]
    x_ld = DSem(newsem("x_ld"))
    x_st = DSem(newsem("x_st"))
    misc_s = DSem(newsem("misc_s"))
    kv_s = DSem(newsem("kv_s"))
    vv_s = DSem(newsem("vv_s"))
    vscr_s = DSem(newsem("vscr_s"))
    sst_s = DSem(newsem("sst_s"))
    scr_s = DSem(newsem("scr_s"))

    state = {"bank": 0, "wb": 0, "gb": 0}

    free_banks = list(range(8))

    def next_bank():
        assert free_banks, "out of PSUM banks"
        return free_banks.pop(0)

    def rel_bank(b):
        assert b not in free_banks
        free_banks.append(b)

    def next_wb():
        b = state["wb"]
        state["wb"] = (b + 1) % NW
        return b

    def psbf(b):
        return psum[b][:].bitcast(BF16)

    class Arena:
        def __init__(self, base_ap_f32, nbytes, tag):
            self.ap = base_ap_f32
            self.n = nbytes
            self.off = 0
            self.tag = tag
            self.items = []

        def alloc(self, name, shape_free, dt):
            esz = 4 if dt == F32 else 2
            nel = int(np.prod(shape_free))
            nb = nel * esz
            nb4 = (nb + 3) // 4 * 4
            assert self.off + nb4 <= self.n, (self.tag, name, self.off, nb4, self.n)
            v = self.ap[:, self.off // 4:(self.off + nb4) // 4]
            if dt != F32:
                v = v.bitcast(dt)
            if len(shape_free) == 2:
                v = v.rearrange("p (a b) -> p a b", a=shape_free[0])
            elif len(shape_free) == 3:
                v = v.rearrange("p (a b c) -> p a b c", a=shape_free[0], b=shape_free[1])
            self.off += nb4
            r = Res(self.tag + "." + name)
            self.items.append(r)
            return v, r

    def dma(eng, out, in_, reads, writes, dsem):
        return S.op(eng, lambda e, o=out, i=in_: e.dma_start(out=o, in_=i), reads=reads, writes=writes, dsem=dsem)

    c_r = res("consts")
    dma("sp", ident[:], ident_d, [], [c_r], misc_s)
    dma("sp", masks[:], masks_d, [], [c_r], misc_s)
    dma("sp", lbt[:], lbl_d, [], [c_r], misc_s)
    dma("sp", hgs[:], hg_d, [], [c_r], misc_s)
    dma("sp", cgs[:], cg_d, [], [c_r], misc_s)
    dma("sp", cws[:], cw_d, [], [c_r], misc_s)
    S.op("dve", lambda e: e.memset(ones[:], 1.0), writes=[res("ones")])
    S.op("dve", lambda e: e.memset(eps_t[:], EPS), writes=[res("eps")])
    eps_r = res("eps")
    ones_r = res("ones")

    lb_r = res("lb")
    lbv = lbt[:].rearrange("p l h -> p h l")
    S.op("dve", lambda e: e.tensor_reduce(out=lbm[:], in_=lbv, axis=AX.X, op=ALU.max), reads=[c_r], writes=[lb_r])
    for l in range(L):
        S.op("dve", lambda e, l=l: e.tensor_tensor(out=lbe[:, l, :], in0=lbt[:, l, :], in1=lbm[:], op=ALU.subtract),
             reads=[c_r, lb_r], writes=[lb_r])
    S.op("act", lambda e: e.activation(out=lbe[:], in_=lbe[:], func=AF.Exp), reads=[lb_r], writes=[lb_r])
    S.op("dve", lambda e: e.tensor_reduce(out=lbm[:], in_=lbe[:].rearrange("p l h -> p h l"), axis=AX.X, op=ALU.add),
         reads=[lb_r], writes=[lb_r])
    S.op("dve", lambda e: e.reciprocal(out=lbm[:], in_=lbm[:]), reads=[lb_r], writes=[lb_r])
    S.op("dve", lambda e: e.memset(lb[:, 0, :], 0.0), reads=[lb_r], writes=[lb_r])
    for l in range(1, L):
        S.op("dve", lambda e, l=l: e.tensor_tensor(out=lbe[:, l, :], in0=lbe[:, l, :], in1=lbm[:], op=ALU.mult),
             reads=[lb_r], writes=[lb_r])
        S.op("dve", lambda e, l=l: e.tensor_tensor(out=lb[:, l, :], in0=lb[:, l - 1, :], in1=lbe[:, l, :], op=ALU.add),
             reads=[lb_r], writes=[lb_r])
    S.op("dve", lambda e: e.tensor_scalar(out=oml[:], in0=lb[:], scalar1=-1.0, scalar2=1.0, op0=ALU.mult, op1=ALU.add),
         reads=[lb_r], writes=[lb_r])

    def load_gb(gname, l):
        i = state["gb"]
        state["gb"] = (i + 1) % 2
        dma("sp", gbuf[i][:], g_d[gname][l:l + 1, :].partition_broadcast(128), [], [gb_r[i]], gb_s[i])
        return i

    ssq_r = [res("ssq%d" % i) for i in range(NST)]
    junk_r = [res("junk%d" % i) for i in range(4)]
    spre_r = [res("spre%d" % i) for i in range(NST)]
    rpost_r = [res("rpost%d" % i) for i in range(NST)]
    rpre_r = [res("rpre%d" % i) for i in range(NST)]

    def boundary(post, pre, src=xs, src_r=xs_r, dst=hT, dst_r=hT_r):
        ybv_ = yb[:].rearrange("p (s d) -> p s d", s=NST)
        gpre = load_gb(pre[0], pre[1]) if pre is not None else None

        def post_stats(st):
            gi, fused = post
            if fused:
                S.op("dve", lambda e: e.tensor_reduce(out=sst2[:, st:st + 1], in_=ssq[:, st, :], axis=AX.X, op=ALU.add),
                     reads=[ssq_r[st]], writes=[rpost_r[st]])
                ssrc = sst2[:, st:st + 1]
            else:
                S.op("act", lambda e: e.activation(out=junk, in_=ybv_[:, st, :], func=AF.Square,
                                                   accum_out=sst2[:, st:st + 1]), reads=[yb_r[st]], writes=[rpost_r[st]] + junk_r)
                ssrc = sst2[:, st:st + 1]
            S.op("act", lambda e: e.activation(out=lnp[:, st:st + 1], in_=ssrc, func=AF.Ln, scale=1.0 / D, bias=eps_t[:]),
                 reads=[rpost_r[st], eps_r], writes=[rpost_r[st]])
            S.op("act", lambda e: e.activation(out=rpost[:, st:st + 1], in_=lnp[:, st:st + 1], func=AF.Exp, scale=-0.5),
                 reads=[rpost_r[st]], writes=[rpost_r[st]])

        def do_post(st):
            gi, fused = post
            if fused:
                S.op("dve", lambda e: e.scalar_tensor_tensor(
                    out=xs[:, st, :], in0=ybv_[:, st, :], scalar=rpost[:, st:st + 1], in1=xs[:, st, :],
                    op0=ALU.mult, op1=ALU.add), reads=[yb_r[st], rpost_r[st], xs_r[st]], writes=[xs_r[st]])
            else:
                S.op("dve", lambda e: e.scalar_tensor_tensor(
                    out=ybv_[:, st, :], in0=ybv_[:, st, :], scalar=rpost[:, st:st + 1], in1=gbuf[gi][:],
                    op0=ALU.mult, op1=ALU.mult), reads=[yb_r[st], rpost_r[st], gb_r[gi]], writes=[yb_r[st]])
                S.op("dve", lambda e: e.tensor_tensor(out=xs[:, st, :], in0=xs[:, st, :], in1=ybv_[:, st, :], op=ALU.add),
                     reads=[yb_r[st], xs_r[st]], writes=[xs_r[st]])

        def pre_a(st):
            S.op("act", lambda e: e.activation(out=junk, in_=src[:, st, :], func=AF.Square,
                                               accum_out=ss[:, st, 0:1]), reads=[src_r[st]], writes=[spre_r[st]] + junk_r)
            S.op("act", lambda e: e.activation(out=lnv[:, st:st + 1], in_=ss[:, st, 0:1], func=AF.Ln, scale=1.0 / D, bias=eps_t[:]),
                 reads=[spre_r[st], eps_r], writes=[rpre_r[st]])
            S.op("act", lambda e: e.activation(out=rstd[:, st:st + 1], in_=lnv[:, st:st + 1], func=AF.Exp, scale=-0.5),
                 reads=[rpre_r[st]], writes=[rpre_r[st]])

        def pre_b(st):
            xi = st % 2
            S.op("dve", lambda e: e.scalar_tensor_tensor(
                out=xn[xi][:], in0=src[:, st, :], scalar=rstd[:, st:st + 1], in1=gbuf[gpre][:],
                op0=ALU.mult, op1=ALU.mult), reads=[src_r[st], rpre_r[st], gb_r[gpre]], writes=[xn_r[xi]])
            for half in range(2):
                b = next_bank()
                pv = psbf(b)
                for j in range(8):
                    kc = half * 8 + j
                    S.op("pe", lambda e, pv=pv, j=j, kc=kc: e.transpose(
                        out=pv[:, j * 128:(j + 1) * 128], in_=xn[xi][:, kc * 128:(kc + 1) * 128], identity=ident[:]),
                        reads=[xn_r[xi], c_r], writes=[ps_r[b]], selfsync=False)
                src_v = pv.rearrange("p (a b) -> p a b", a=8)
                dst_v = dst[:, half * 8:(half + 1) * 8, st * 128:(st + 1) * 128]
                if half == 0:
                    S.op("act", lambda e, s_=src_v, d=dst_v: e.copy(out=d, in_=s_), reads=[ps_r[b]], writes=[dst_r[st]])
                else:
                    S.op("dve", lambda e, s_=src_v, d=dst_v: e.tensor_copy(out=d, in_=s_), reads=[ps_r[b]], writes=[dst_r[st]])
                rel_bank(b)

        if post is not None:
            for st in range(NST):
                post_stats(st)
        for step in range(NST + 1):
            if step < NST:
                if post is not None:
                    do_post(step)
                if pre is not None:
                    pre_a(step)
            if step >= 1 and pre is not None:
                pre_b(step - 1)

    def load_w(w_ap2d, ncols):
        i = next_wb()
        dma("pool", wbuf[i][:, :, 0:ncols], w_ap2d.rearrange("(k p) n -> p k n", p=128), [], [wb_r[i]], wb_s[i])
        return i

    def proj_fm(w_l, col0, ncols, consumer, src=hT, src_r=hT_r, ntok=T):
        wi = load_w(w_l[:, col0:col0 + ncols], ncols)
        for oc in range(ncols // 128):
            b = next_bank()
            for k in range(KC):
                S.op("pe", lambda e, b=b, k=k, oc=oc, wi=wi: e.matmul(
                    psum[b][:, 0:ntok], lhsT=wbuf[wi][:, k, oc * 128:(oc + 1) * 128], rhs=src[:, k, 0:ntok],
                    start=(k == 0), stop=(k == KC - 1)),
                    reads=[wb_r[wi]] + src_r, writes=[ps_r[b]], selfsync=False)
            consumer(oc, b)
            rel_bank(b)

    def proj_tm(w_l, nK, act, act_r, mode="fused", gi=None):
        ybv_ = yb[:].rearrange("p (s d) -> p s d", s=NST)
        for cq in range(4):
            banks = [next_bank() for _ in range(NST)]
            for kg in range(nK // KC):
                wi = load_w(w_l[kg * D:(kg + 1) * D, cq * 512:(cq + 1) * 512], 512)
                for st in range(NST):
                    for kk in range(KC):
                        k = kg * KC + kk
                        S.op("pe", lambda e, b=banks[st], k=k, kk=kk, st=st, wi=wi: e.matmul(
                            psum[b][:], lhsT=act[:, k, st * 128:(st + 1) * 128], rhs=wbuf[wi][:, kk, :],
                            start=(k == 0), stop=(k == nK - 1)),
                            reads=[wb_r[wi]] + act_r[st], writes=[ps_r[banks[st]]], selfsync=False)
            for st in range(NST):
                b = banks[st]
                dst = ybv_[:, st, cq * 512:(cq + 1) * 512]
                if mode == "fused":
                    S.op("dve", lambda e, b=b, d=dst, cq=cq: e.tensor_tensor(
                        out=d, in0=psum[b][:], in1=gbuf[gi][:, cq * 512:(cq + 1) * 512], op=ALU.mult),
                        reads=[ps_r[b], gb_r[gi]], writes=[yb_r[st]])
                    S.op("act", lambda e, b=b, st=st, cq=cq: e.activation(
                        out=junk[:, st * 512:(st + 1) * 512], in_=psum[b][:], func=AF.Square, accum_out=ssq[:, st, cq:cq + 1]),
                        reads=[ps_r[b], yb_r[st]], writes=[ssq_r[st], junk_r[st]])
                elif mode == "accum":
                    S.op("dve", lambda e, b=b, d=dst: e.tensor_tensor(out=d, in0=d, in1=psum[b][:], op=ALU.add),
                         reads=[ps_r[b], yb_r[st]], writes=[yb_r[st]])
                else:
                    S.op("dve", lambda e, b=b, d=dst: e.tensor_copy(out=d, in_=psum[b][:]),
                         reads=[ps_r[b]], writes=[yb_r[st]])
                rel_bank(b)

    ocT = Bb[:, 0:KC, :]
    misc_ap = Bb[:, KC:32, :].rearrange("p a b -> p (a b)").bitcast(F32)

    bm_r = res("Bmisc")
    sst_r = res("Sst")
    tails_r = res("tails")
    ybv = yb[:].rearrange("p (s d) -> p s d", s=NST)
    SCALE = 1.0 / float(np.sqrt(512.0))

    def evac_copy(i, out, in_, reads, writes):
        if i % 2 == 0:
            S.op("act", lambda e: e.copy(out=out, in_=in_), reads=reads, writes=writes)
        else:
            S.op("dve", lambda e: e.tensor_copy(out=out, in_=in_), reads=reads, writes=writes)

    def prologue():
        ntm = n_seq * NMEM
        nsm = ntm // 128
        dma("sp", xs[:, 0:nsm, :], mem_d.rearrange("(s p) d -> p s d", p=128), [], xs_r, x_ld)
        if nsm < NST:
            S.op("dve", lambda e: e.memset(xs[:, nsm:NST, :], 1.0), reads=[], writes=xs_r)
        vtmp = Bb[:, KC:32, :].rearrange("p a b -> p (a b)").rearrange("p (s d) -> p s d", s=NST)
        for l in range(L):
            boundary(None, ("g_mem", l))
            for u in range(4):
                def cons(oc, b, u=u):
                    kc = u * 4 + oc
                    evac_copy(oc, ocT[:, kc, :], psum[b][:], [ps_r[b]], oc_r)
                proj_fm(w_k_d[l], u * 512, 512, cons)
            dma("sp", kT_scr[l], ocT[:, :, 0:ntm], oc_r, [res("kT_scr%d" % l)], scr_s)
            proj_tm(w_v_d[l], KC, hT, [[r] for r in hT_r], mode="plain")
            for st in range(nsm):
                S.op("act", lambda e, st=st: e.copy(out=vtmp[:, st, :], in_=ybv[:, st, :]),
                     reads=[yb_r[st]], writes=[bm_r])
            dma("sp", v_scr[l], vtmp[:, 0:nsm, :], [bm_r], [res("v_scr%d" % l)], vscr_s)
        for l in range(L):
            res("kT_scr%d" % l).lw = (scr_s.sem, scr_s.count, None)
            res("v_scr%d" % l).lw = (vscr_s.sem, vscr_s.count, None)

    def mlp(l):
        act_r = [[oc_r[st], bm_r] for st in range(NST)]
        for half in range(2):
            for u in range(8):
                def cons(oc, b, u=u):
                    fc = u * 4 + oc
                    ri = fc % 2
                    S.op("dve", lambda e: e.tensor_scalar(out=rt[ri], in0=psum[b][:], scalar1=0.0, scalar2=None,
                                                          op0=ALU.max), reads=[ps_r[b]], writes=[rt_r[ri]])
                    S.op("act", lambda e: e.activation(out=Bb[:, fc, :], in_=rt[ri], func=AF.Square),
                         reads=[rt_r[ri]], writes=oc_r + [bm_r])
                proj_fm(w_up_d[l], half * 4096 + u * 512, 512, cons)
            proj_tm(w_dn_d[l][half * 4096:(half + 1) * 4096, :], 32, Bb, act_r, mode=("first" if half == 0 else "accum"))
        return (load_gb("g_mlp_post", l), False)

    def attn(l, seq):
        A1 = Arena(yb[:], 32768, "at1")
        kTs, kTs_r = A1.alloc("kT", [KC, NMEM], BF16)
        Vs, Vs_r = A1.alloc("V", [2, D], BF16)
        qT, qT_r = A1.alloc("qT", [KC, T], BF16)
        A2 = Arena(misc_ap, 16384, "at2")
        ex, ex_r = A2.alloc("ex", [4, NMEM], F32)
        Pn, Pn_r = A2.alloc("Pn", [4, NMEM], BF16)
        PT, PT_r = A2.alloc("PT", [4, 2, T], BF16)
        S.handoff(yb_r, A1.items)
        S.handoff([bm_r], A2.items)
        dma("sp", kTs, kT_scr[l][:, :, seq * NMEM:(seq + 1) * NMEM], [res("kT_scr%d" % l)], [kTs_r], kv_s)
        dma("sp", Vs, v_scr[l][:, 2 * seq:2 * seq + 2, :], [res("v_scr%d" % l)], [Vs_r], vv_s)
        for u in range(4):
            def cons(oc, b, u=u):
                evac_copy(oc, qT[:, u * 4 + oc, :], psum[b][:], [ps_r[b]], [qT_r])
            proj_fm(w_q_d[l], u * 512, 512, cons)
        st_r = res("attn_stats")
        def emit_scores(st):
            banks = [next_bank(), next_bank()]
            for head in range(4):
                b = banks[head // 2]
                for dc in range(4):
                    c = head * 4 + dc
                    S.op("pe", lambda e, b=b, c=c, head=head, dc=dc, st=st: e.matmul(
                        psum[b][:, (head % 2) * NMEM:(head % 2 + 1) * NMEM], lhsT=qT[:, c, st * 128:(st + 1) * 128],
                        rhs=kTs[:, c, :], start=(dc == 0), stop=(dc == 3)),
                        reads=[qT_r, kTs_r], writes=[ps_r[b]], selfsync=False)
            return banks

        sbanks = {0: emit_scores(0), 1: emit_scores(1)}
        for st in range(NST):
            if st + 2 < NST:
                sbanks[st + 2] = emit_scores(st + 2)
            banks = sbanks[st]
            for i in range(2):
                b = banks[i]
                S.op("dve", lambda e, b=b, i=i: e.tensor_reduce(
                    out=amax[:, 2 * i:2 * i + 2], in_=psum[b][:].rearrange("p (a n) -> p a n", a=2), axis=AX.X, op=ALU.max),
                    reads=[ps_r[b]], writes=[st_r])
            S.op("dve", lambda e: e.tensor_scalar(out=anm[:], in0=amax[:], scalar1=-SCALE, scalar2=None, op0=ALU.mult),
                 reads=[st_r], writes=[st_r])
            for head in range(4):
                b = banks[head // 2]
                S.op("act", lambda e, b=b, head=head: e.activation(
                    out=ex[:, head, :], in_=psum[b][:, (head % 2) * NMEM:(head % 2 + 1) * NMEM], func=AF.Exp,
                    scale=SCALE, bias=anm[:, head:head + 1], accum_out=asum[:, head:head + 1]),
                    reads=[ps_r[b], st_r], writes=[ex_r, st_r])
            S.op("dve", lambda e: e.reciprocal(out=arinv[:], in_=asum[:]), reads=[st_r], writes=[st_r])
            rel_bank(banks[0])
            rel_bank(banks[1])
            for head in range(4):
                S.op("dve", lambda e, head=head: e.tensor_scalar(
                    out=Pn[:, head, :], in0=ex[:, head, :], scalar1=arinv[:, head:head + 1], scalar2=None, op0=ALU.mult),
                    reads=[ex_r, st_r], writes=[Pn_r])
            bt = next_bank()
            pv = psbf(bt)
            for head in range(4):
                for ncn in range(2):
                    j = head * 2 + ncn
                    S.op("pe", lambda e, pv=pv, j=j, head=head, ncn=ncn: e.transpose(
                        out=pv[:, j * 128:(j + 1) * 128], in_=Pn[:, head, ncn * 128:(ncn + 1) * 128], identity=ident[:]),
                        reads=[Pn_r, c_r], writes=[ps_r[bt]], selfsync=False)
            S.op("act", lambda e, pv=pv, st=st: e.copy(
                out=PT[:, :, :, st * 128:(st + 1) * 128], in_=pv.rearrange("p (h c t) -> p h c t", h=4, c=2)),
                reads=[ps_r[bt]], writes=[PT_r])
            rel_bank(bt)
        for head in range(4):
            for dc in range(4):
                c = head * 4 + dc
                b = next_bank()
                for ncn in range(2):
                    S.op("pe", lambda e, b=b, c=c, head=head, ncn=ncn: e.matmul(
                        psum[b][:], lhsT=Vs[:, ncn, c * 128:(c + 1) * 128], rhs=PT[:, head, ncn, :],
                        start=(ncn == 0), stop=(ncn == 1)),
                        reads=[Vs_r, PT_r], writes=[ps_r[b]], selfsync=False)
                evac_copy(c, ocT[:, c, :], psum[b][:], [ps_r[b]], oc_r)
                rel_bank(b)
        S.handoff(A1.items, yb_r)
        S.handoff(A2.items, [bm_r])
        gi = load_gb("g_x_post", l)
        proj_tm(w_xo_d[l], KC, ocT, [[r] for r in oc_r], mode="fused", gi=gi)
        return (gi, True)

    def mixer(l, first_in_seq):
        if first_in_seq:
            S.op("dve", lambda e: e.memset(Sst[:], 0.0), reads=[], writes=[sst_r])
            S.op("dve", lambda e: e.memset(tails[:, l, :, :], 0.0), reads=[], writes=[tails_r])
        else:
            dma("sp", Sst[:].rearrange("p a b -> p (a b)"), s_scr[l], [res("s_scr%d" % l)], [sst_r], sst_s)
        A1 = Arena(yb[:], 32768, "mx1")
        A2 = Arena(misc_ap, 16384, "mx2")
        sgf = [A1.alloc("sgf%d" % i, [T], F32) for i in range(2)]
        qs = [A1.alloc("qs%d" % i, [T], F32) for i in range(2)]
        vsb = [A1.alloc("vsb%d" % i, [NST, 128], BF16) for i in range(2)]
        sgt = [A1.alloc("sgt%d" % i, [T], F32) for i in range(2)]
        X1, X1_r = A1.alloc("X1", [T], F32)
        X2, X2_r = A1.alloc("X2", [T], F32)
        X3, X3_r = A1.alloc("X3", [T], F32)
        X45, X45_r = A1.alloc("X45", [2 * T], F32)
        X4, X5 = X45[:, 0:T], X45[:, T:2 * T]
        X4_r = X5_r = X45_r
        Q64, Q64_r = A1.alloc("Q64", [T], BF16)
        KH, KH_r = A1.alloc("KH", [T], BF16)
        Q0, Q0_r = A1.alloc("Q0", [T], BF16)
        K0, K0_r = A1.alloc("K0", [T], BF16)
        khTlo, khTlo_r = A1.alloc("khTlo", [NST, 128], BF16)
        khThi, khThi_r = A1.alloc("khThi", [NST, 128], BF16)
        Sall, Sall_r = A2.alloc("Sall", [7, 128], F32)
        Sbf, Sbf_r = A2.alloc("Sbf", [8, 128], BF16)
        sq, sq_r = A2.alloc("sq", [T], F32)
        rb, rb_r = A2.alloc("rb", [T], F32)
        t1, t1_r = A2.alloc("t1", [T], F32)
        scm, scm_r = A2.alloc("scm", [NST, 128], BF16)
        K16, K16_r = A2.alloc("K16", [T], BF16)
        Q2, Q2_r = A2.alloc("Q2", [T], BF16)
        K32, K32_r = A2.alloc("K32", [T], BF16)
        lf, lf_r = X1, X1_r
        bcs, bcs_r = X3, X3_r
        km, km_r = X2, X2_r
        ucv, ucv_r = X45[:, 0:T + 2], X45_r
        qt, qt_r = Q64, Q64_r
        S.handoff(yb_r, A1.items)
        S.handoff([bm_r], A2.items)
        S.op("dve", lambda e: e.memset(khTlo[64:128, :, :], 0.0), reads=[], writes=[khTlo_r])
        S.op("dve", lambda e: e.memset(khThi[0:64, :, :], 0.0), reads=[], writes=[khThi_r])
        el_r = res("elast")
        wl = w_in_d[l]

        def Pa(h):
            wi = load_w(wl[:, h * 512:(h + 1) * 512], 512)
            bk = {"wi": wi}
            for name, oc in (("q", 0), ("f", 1), ("g", 3)):
                b = next_bank()
                bk[name] = b
                for k in range(KC):
                    S.op("pe", lambda e, b=b, k=k, oc=oc, wi=wi: e.matmul(
                        psum[b][:], lhsT=wbuf[wi][:, k, oc * 128:(oc + 1) * 128], rhs=hT[:, k, :],
                        start=(k == 0), stop=(k == KC - 1)), reads=[wb_r[wi]] + hT_r, writes=[ps_r[b]], selfsync=False)
            return bk

        def Pb(bk):
            wi = bk.pop("wi")
            b = next_bank()
            bk["v"] = b
            for st in range(NST):
                for k in range(KC):
                    S.op("pe", lambda e, b=b, k=k, st=st, wi=wi: e.matmul(
                        psum[b][:, st * 128:(st + 1) * 128], lhsT=hT[:, k, st * 128:(st + 1) * 128],
                        rhs=wbuf[wi][:, k, 256:384], start=(k == 0), stop=(k == KC - 1)),
                        reads=[wb_r[wi], hT_r[st]], writes=[ps_r[b]], selfsync=False)
            return bk

        def E(h, bk):
            p = h % 2
            S.op("act", lambda e: e.activation(out=sgf[p][0], in_=psum[bk["f"]][:], func=AF.Sigmoid),
                 reads=[ps_r[bk["f"]]], writes=[sgf[p][1]])
            S.op("act", lambda e: e.activation(out=qs[p][0], in_=psum[bk["q"]][:], func=AF.Silu),
                 reads=[ps_r[bk["q"]]], writes=[qs[p][1]])
            S.op("act", lambda e: e.activation(out=sgt[p][0], in_=psum[bk["g"]][:], func=AF.Silu),
                 reads=[ps_r[bk["g"]]], writes=[sgt[p][1]])
            S.op("dve", lambda e: e.tensor_copy(out=vsb[p][0], in_=psum[bk["v"]][:].rearrange("p (s v) -> p s v", s=NST)),
                 reads=[ps_r[bk["v"]]], writes=[vsb[p][1]])
            for b in bk.values():
                rel_bank(b)

        def Rh(h, mid=None):
            p = h % 2
            f, f_r = sgf[p]
            qsp, qsp_r = qs[p]
            v3 = lambda ap, c: ap.rearrange("p (c t) -> p c t", c=c)
            S.op("dve", lambda e: e.tensor_scalar(out=f, in0=f, scalar1=oml[:, l, h:h + 1], scalar2=lb[:, l, h:h + 1],
                                                  op0=ALU.mult, op1=ALU.add), reads=[f_r, lb_r], writes=[f_r])
            S.op("act", lambda e: e.activation(out=X1, in_=f, func=AF.Ln), reads=[f_r], writes=[X1_r])
            S.op("dve", lambda e: e.tensor_scalar(out=X2, in0=f, scalar1=-1.0, scalar2=1.0, op0=ALU.mult, op1=ALU.add),
                 reads=[f_r], writes=[X2_r])
            for dst, dst_r, mi in ((X3, X3_r, 0), (X4, X4_r, 1), (X5, X5_r, 2)):
                S.op("dve", lambda e, dst=dst, mi=mi: e.tensor_tensor_scan(out=dst, data0=masks[:, mi, :], data1=X1, initial=0.0,
                                                                           op0=ALU.mult, op1=ALU.add),
                     reads=[X1_r, c_r], writes=[dst_r])
            S.op("dve", lambda e: e.tensor_scalar(out=X4, in0=X4, scalar1=-80.0, scalar2=None, op0=ALU.max),
                 reads=[X4_r], writes=[X4_r])
            S.op("act", lambda e: e.activation(out=X1, in_=X3, func=AF.Exp), reads=[X3_r], writes=[X1_r])
            S.op("dve", lambda e: e.tensor_copy(out=elast[:], in_=v3(X1, 8)[:, :, 63]), reads=[X1_r], writes=[el_r])
            S.op("dve", lambda e: e.tensor_tensor(out=Q64, in0=qsp, in1=X1, op=ALU.mult), reads=[qsp_r, X1_r], writes=[Q64_r])
            S.op("dve", lambda e: e.tensor_copy(out=bl64[:], in_=v3(X3, 8)[:, :, 63]), reads=[X3_r], writes=[el_r])
            S.op("dve", lambda e: e.tensor_tensor(out=v3(X3, 8), in0=v3(X3, 8),
                                                  in1=bl64[:].unsqueeze(2).broadcast_to([128, 8, 64]), op=ALU.subtract),
                 reads=[X3_r, el_r], writes=[X3_r])
            S.op("act", lambda e: e.activation(out=X3, in_=X3, func=AF.Exp, scale=-1.0), reads=[X3_r], writes=[X3_r])
            S.op("dve", lambda e: e.tensor_tensor(out=KH, in0=X2, in1=X3, op=ALU.mult), reads=[X2_r, X3_r], writes=[KH_r])
            S.op("act", lambda e: e.activation(out=X1, in_=X4, func=AF.Exp), reads=[X4_r], writes=[X1_r])
            S.op("dve", lambda e: e.tensor_copy(out=el16[:], in_=v3(X1, 32)[:, :, 15]), reads=[X1_r], writes=[el_r])
            S.op("dve", lambda e: e.tensor_tensor(out=Q0, in0=qsp, in1=X1, op=ALU.mult), reads=[qsp_r, X1_r], writes=[Q0_r])
            S.op("act", lambda e: e.activation(out=X4, in_=X4, func=AF.Exp, scale=-1.0), reads=[X4_r], writes=[X4_r])
            S.op("dve", lambda e: e.tensor_tensor(out=X4, in0=X2, in1=X4, op=ALU.mult), reads=[X2_r, X4_r], writes=[X4_r])
            S.op("act", lambda e: e.copy(out=K0, in_=X4), reads=[X4_r], writes=[K0_r])
            S.op("dve", lambda e: e.tensor_tensor(out=v3(K16, 32), in0=v3(X4, 32),
                                                  in1=el16[:].unsqueeze(2).broadcast_to([128, 32, 16]), op=ALU.mult),
                 reads=[X4_r, el_r], writes=[K16_r])
            S.op("act", lambda e: e.activation(out=X1, in_=X5, func=AF.Exp), reads=[X5_r], writes=[X1_r])
            S.op("dve", lambda e: e.tensor_tensor(out=Q2, in0=qsp, in1=X1, op=ALU.mult), reads=[qsp_r, X1_r], writes=[Q2_r])
            S.op("dve", lambda e: e.tensor_copy(out=bl32[:], in_=v3(X5, 16)[:, :, 31]), reads=[X5_r], writes=[el_r])
            S.op("dve", lambda e: e.tensor_tensor(out=v3(X5, 16), in0=v3(X5, 16),
                                                  in1=bl32[:].unsqueeze(2).broadcast_to([128, 16, 32]), op=ALU.subtract),
                 reads=[X5_r, el_r], writes=[X5_r])
            S.op("act", lambda e: e.activation(out=X5, in_=X5, func=AF.Exp, scale=-1.0), reads=[X5_r], writes=[X5_r])
            S.op("dve", lambda e: e.tensor_tensor(out=K32, in0=X2, in1=X5, op=ALU.mult), reads=[X2_r, X5_r], writes=[K32_r])
            bT = next_bank()
            pv = psbf(bT)
            for st in range(NST):
                S.op("pe", lambda e, st=st: e.transpose(out=pv[:, st * 128:(st + 1) * 128], in_=KH[:, st * 128:(st + 1) * 128],
                                                        identity=ident[:]), reads=[KH_r, c_r], writes=[ps_r[bT]], selfsync=False)
            S.op("act", lambda e: e.copy(out=khTlo[0:64, :, :], in_=pv[0:64, 0:T].rearrange("p (s k) -> p s k", s=NST)),
                 reads=[ps_r[bT]], writes=[khTlo_r])
            S.op("act", lambda e: e.copy(out=khThi[64:128, :, :], in_=pv[64:128, 0:T].rearrange("p (s k) -> p s k", s=NST)),
                 reads=[ps_r[bT]], writes=[khThi_r])
            rel_bank(bT)
            bP = []
            for (kk, kk_r, qq, qq_r) in ((K0, K0_r, Q0, Q0_r), (K16, K16_r, Q0, Q0_r), (K32, K32_r, Q2, Q2_r)):
                b = next_bank()
                bP.append(b)
                for st in range(NST):
                    S.op("pe", lambda e, st=st, b=b, kk=kk, qq=qq: e.matmul(
                        psum[b][:, st * 128:(st + 1) * 128], lhsT=kk[:, st * 128:(st + 1) * 128],
                        rhs=qq[:, st * 128:(st + 1) * 128], start=True, stop=True),
                        reads=[kk_r, qq_r], writes=[ps_r[b]], selfsync=False)
            S.op("dve", lambda e: e.tensor_tensor(out=sq, in0=psum[bP[0]][:], in1=masks[:, 3, :], op=ALU.mult),
                 reads=[ps_r[bP[0]], c_r], writes=[sq_r])
            S.op("dve", lambda e: e.tensor_tensor(out=t1, in0=psum[bP[1]][:], in1=masks[:, 4, :], op=ALU.mult),
                 reads=[ps_r[bP[1]], c_r], writes=[t1_r])
            S.op("dve", lambda e: e.tensor_tensor(out=sq, in0=sq, in1=t1, op=ALU.add), reads=[sq_r, t1_r], writes=[sq_r])
            S.op("dve", lambda e: e.tensor_tensor(out=t1, in0=psum[bP[2]][:], in1=masks[:, 5, :], op=ALU.mult),
                 reads=[ps_r[bP[2]], c_r], writes=[t1_r])
            S.op("dve", lambda e: e.tensor_tensor(out=scm.rearrange("p s t -> p (s t)"), in0=sq, in1=t1, op=ALU.add),
                 reads=[sq_r, t1_r], writes=[scm_r])
            for b in bP:
                rel_bank(b)
            bU = [next_bank(), next_bank()]
            for c in range(8):
                st, hf = c // 2, c % 2
                S.op("pe", lambda e, c=c, st=st, hf=hf: e.matmul(
                    psum[bU[c // 4]][:, (c % 4) * 128:(c % 4 + 1) * 128], lhsT=(khTlo if hf == 0 else khThi)[:, st, :],
                    rhs=vsb[p][0][:, st, :], start=True, stop=True),
                    reads=[khTlo_r, khThi_r, vsb[p][1]], writes=[ps_r[bU[c // 4]]], selfsync=False)
            S.op("act", lambda e: e.copy(out=Sbf[:, 0, :], in_=Sst[:, h, :]), reads=[sst_r], writes=[Sbf_r])
            for c in range(8):
                cur = Sst[:, h, :] if c == 0 else Sall[:, c - 1, :]
                out = Sst[:, h, :] if c == 7 else Sall[:, c, :]
                rd = [ps_r[bU[c // 4]], el_r, sst_r if c == 0 else Sall_r]
                wr = [sst_r] if c == 7 else [Sall_r]
                S.op("dve", lambda e, c=c, cur=cur, out=out: e.scalar_tensor_tensor(
                    out=out, in0=cur, scalar=elast[:, c:c + 1], in1=psum[bU[c // 4]][:, (c % 4) * 128:(c % 4 + 1) * 128],
                    op0=ALU.mult, op1=ALU.add), reads=rd, writes=wr)
            S.op("act", lambda e: e.copy(out=Sbf[:, 1:8, :], in_=Sall[:, 0:7, :]), reads=[Sall_r], writes=[Sbf_r])
            rel_bank(bU[0])
            rel_bank(bU[1])
            if mid is not None:
                mid()
            bO = next_bank()
            for st in range(NST):
                S.op("pe", lambda e, st=st: e.matmul(psum[bO][:, st * 128:(st + 1) * 128], lhsT=vsb[p][0][:, st, :],
                                                     rhs=scm[:, st, :], start=True, stop=False),
                     reads=[vsb[p][1], scm_r], writes=[ps_r[bO]], selfsync=False)
                for c in (2 * st, 2 * st + 1):
                    S.op("pe", lambda e, c=c: e.matmul(psum[bO][:, c * 64:(c + 1) * 64], lhsT=Sbf[:, c, :],
                                                       rhs=qt[:, c * 64:(c + 1) * 64], start=False, stop=(c % 2 == 1)),
                         reads=[Sbf_r, qt_r], writes=[ps_r[bO]], selfsync=False)
            groupnorm_out(bO, None, hgs[:, l, h:h + 1], sgt[p], h)
            rel_bank(bO)

        def groupnorm_out(bO, z_sb, gcol, gate, chunk):
            if z_sb is None:
                src, src_r = psum[bO][:], ps_r[bO]
            else:
                src, src_r = z_sb
            S.op("act", lambda e: e.activation(out=sq, in_=src, func=AF.Square), reads=[src_r], writes=[sq_r])
            bN = next_bank()
            S.op("pe", lambda e: e.matmul(psum[bN][:], lhsT=ones[:], rhs=sq, start=True, stop=True),
                 reads=[sq_r, ones_r], writes=[ps_r[bN]], selfsync=False)
            S.op("act", lambda e: e.activation(out=rb, in_=psum[bN][:], func=AF.Ln, scale=1.0 / 128, bias=eps_t[:]),
                 reads=[ps_r[bN], eps_r], writes=[rb_r])
            S.op("act", lambda e: e.activation(out=rb, in_=rb, func=AF.Exp, scale=-0.5), reads=[rb_r], writes=[rb_r])
            rel_bank(bN)
            if gate is None:
                S.op("dve", lambda e: e.scalar_tensor_tensor(out=ocT[:, chunk, :], in0=src, scalar=gcol, in1=rb,
                                                             op0=ALU.mult, op1=ALU.mult),
                     reads=[src_r, rb_r, c_r], writes=oc_r)
            else:
                S.op("dve", lambda e: e.tensor_tensor(out=t1, in0=src, in1=rb, op=ALU.mult), reads=[src_r, rb_r], writes=[t1_r])
                S.op("dve", lambda e: e.scalar_tensor_tensor(out=ocT[:, chunk, :], in0=t1, scalar=gcol, in1=gate[0],
                                                             op0=ALU.mult, op1=ALU.mult),
                     reads=[t1_r, gate[1], c_r], writes=oc_r)

        def Pc(j):
            wi = load_w(wl[:, 4096 + j * 384:4096 + (j + 1) * 384], 384)
            bk = []
            for oc in range(3):
                b = next_bank()
                bk.append(b)
                for k in range(KC):
                    S.op("pe", lambda e, b=b, k=k, oc=oc, wi=wi: e.matmul(
                        psum[b][:], lhsT=wbuf[wi][:, k, oc * 128:(oc + 1) * 128], rhs=hT[:, k, :],
                        start=(k == 0), stop=(k == KC - 1)), reads=[wb_r[wi]] + hT_r, writes=[ps_r[b]], selfsync=False)
            return bk

        def Rc(j, bk):
            bB, bC, bH = bk
            ccs, ccs_r = lf, lf_r
            ycv, ycv_r = bcs, bcs_r
            z, z_r = km, km_r
            S.op("act", lambda e: e.copy(out=ccs, in_=psum[bC][:]), reads=[ps_r[bC]], writes=[ccs_r])
            S.op("act", lambda e: e.copy(out=ucv[:, 0:2], in_=tails[:, l, j, :]), reads=[tails_r], writes=[ucv_r])
            S.op("dve", lambda e: e.tensor_tensor(out=ucv[:, 2:T + 2], in0=ccs, in1=psum[bH][:], op=ALU.mult),
                 reads=[ccs_r, ps_r[bH], ucv_r], writes=[ucv_r])
            S.op("act", lambda e: e.copy(out=tails[:, l, j, :], in_=ucv[:, T:T + 2]), reads=[ucv_r], writes=[tails_r])
            S.op("dve", lambda e: e.tensor_scalar(out=ycv, in0=ucv[:, 0:T], scalar1=cws[:, l, 0, j:j + 1], scalar2=None,
                                                  op0=ALU.mult), reads=[ucv_r, c_r], writes=[ycv_r])
            for tap in (1, 2):
                S.op("dve", lambda e, tap=tap: e.scalar_tensor_tensor(
                    out=ycv, in0=ucv[:, tap:T + tap], scalar=cws[:, l, tap, j:j + 1], in1=ycv, op0=ALU.mult, op1=ALU.add),
                    reads=[ucv_r, ycv_r, c_r], writes=[ycv_r])
            S.op("dve", lambda e: e.tensor_tensor(out=z, in0=ycv, in1=psum[bB][:], op=ALU.mult),
                 reads=[ycv_r, ps_r[bB]], writes=[z_r])
            for b in bk:
                rel_bank(b)
            groupnorm_out(None, (z, z_r), cgs[:, l, j:j + 1], None, 8 + j)

        bk = Pa(0)
        Pb(bk)
        E(0, bk)
        for h in range(8):
            if h + 1 < 8:
                bk = Pa(h + 1)
                Rh(h, mid=(lambda bk=bk: Pb(bk)))
                E(h + 1, bk)
            else:
                bkc = Pc(0)
                Rh(h)
        for j in range(8):
            cur = bkc
            if j + 1 < 8:
                bkc = Pc(j + 1)
            Rc(j, cur)
        dma("sp", s_scr[l], Sst[:].rearrange("p a b -> p (a b)"), [sst_r], [res("s_scr%d" % l)], sst_s)
        S.handoff(A1.items, yb_r)
        S.handoff(A2.items, [bm_r])
        gi = load_gb("g_mix_post", l)
        proj_tm(w_mo_d[l], KC, ocT, [[r] for r in oc_r], mode="fused", gi=gi)
        return (gi, True)

    if "attn" in phases:
        prologue()
    for tile in range(n_tiles):
        seq = tile // tiles_per_seq
        first_in_seq = (tile % tiles_per_seq == 0)
        dma("sp", xs[:], x_d[tile * T:(tile + 1) * T, :].rearrange("(s p) d -> p s d", p=128), [res("out")], xs_r, x_ld)
        seqb = []
        for l in range(L):
            if "mix" in phases:
                seqb.append(("mix", l, "g_mix_pre"))
            if "attn" in phases:
                seqb.append(("attn", l, "g_x_pre"))
            if "mlp" in phases:
                seqb.append(("mlp", l, "g_mlp_pre"))
        boundary(None, (seqb[0][2], seqb[0][1]))
        for i, (kind, l, _) in enumerate(seqb):
            if kind == "mix":
                post = mixer(l, first_in_seq)
            elif kind == "attn":
                post = attn(l, seq)
            else:
                post = mlp(l)
            nxt = (seqb[i + 1][2], seqb[i + 1][1]) if i + 1 < len(seqb) else None
            boundary(post, nxt)
        dma("sp", out_d[tile * T:(tile + 1) * T, :].rearrange("(s p) d -> p s d", p=128), xs[:], xs_r, [res("out")], x_st)

    with nc.Block() as block:
        S.replay(block, [(x_st.sem, x_st.count)])
    es.close()
    build.stats = {e: len(S.ops[e]) for e in S.ENGS}
    build.stats["waits"] = S.nwaits
    return nc


def _host_consts():
    import ml_dtypes
    ident = np.eye(128, dtype=np.float32).astype(ml_dtypes.bfloat16)
    t = np.arange(T)
    masks = np.zeros((128, 6, T), np.float32)
    masks[:, 0, :] = (t % 64 != 0)[None, :]
    masks[:, 1, :] = (t % 16 != 0)[None, :]
    masks[:, 2, :] = (t % 32 != 0)[None, :]
    s_ = np.arange(128)[:, None]
    tt = np.arange(128)[None, :]
    m0 = (s_ // 16 == tt // 16) & (s_ <= tt)
    m1 = (s_ // 32 == tt // 32) & ((s_ // 16) % 2 == 0) & ((tt // 16) % 2 == 1)
    m2 = (s_ // 64 == tt // 64) & ((s_ // 32) % 2 == 0) & ((tt // 32) % 2 == 1)
    for i, m in enumerate((m0, m1, m2)):
        masks[:, 3 + i, :] = np.tile(m.astype(np.float32), (1, NST))
    return ident, masks.astype(ml_dtypes.bfloat16)


def _perm_w_in(w_in):
    L = w_in.shape[0]
    idx = []
    for h in range(8):
        for part in range(4):
            idx.extend(range(part * 1024 + h * 128, part * 1024 + (h + 1) * 128))
    for j in range(8):
        for part in range(3):
            idx.extend(range(4096 + part * 1024 + j * 128, 4096 + part * 1024 + (j + 1) * 128))
    idx = np.asarray(idx)
    return np.ascontiguousarray(w_in[:, :, idx])


def make_in_maps(inputs, n_cores, n_layers, n_tiles, tiles_per_seq, n_seq):
    f = lambda a: np.ascontiguousarray(np.asarray(a, dtype=np.float32))
    L = n_layers
    ident, masks = _host_consts()
    shared = {
        "w_in": _perm_w_in(f(inputs["w_in"])[:L]),
        "w_mix_out": f(inputs["w_mix_out"])[:L], "w_q": f(inputs["w_q"])[:L], "w_k": f(inputs["w_k"])[:L],
        "w_v": f(inputs["w_v"])[:L], "w_xo": f(inputs["w_xo"])[:L], "w_up": f(inputs["w_up"])[:L],
        "w_down": f(inputs["w_down"])[:L],
        "ident": ident, "masks": masks,
        "lbT": np.ascontiguousarray(f(inputs["hgrn_lb_logits"])[:L].reshape(L, 8, 128).transpose(2, 0, 1)),
        "hgT": np.ascontiguousarray(f(inputs["hgrn_norm_g"])[:L].reshape(L, 8, 128).transpose(2, 0, 1)),
        "cgT": np.ascontiguousarray(f(inputs["conv_norm_g"])[:L].reshape(L, 8, 128).transpose(2, 0, 1)),
        "cwT": np.ascontiguousarray(f(inputs["conv_w"])[:L].reshape(L, 3, 8, 128).transpose(3, 0, 1, 2)),
    }
    for n in ["g_mix_pre", "g_mix_post", "g_x_pre", "g_mem", "g_x_post", "g_mlp_pre", "g_mlp_post"]:
        shared[n] = f(inputs[n])[:L]
    x = f(inputs["x"])
    mem = f(inputs["mem"])
    seq_len = tiles_per_seq * T
    maps = []
    for c in range(n_cores):
        m = dict(shared)
        m["x"] = np.ascontiguousarray(x[c * n_seq:(c + 1) * n_seq, :seq_len].reshape(n_seq * seq_len, D))
        m["mem"] = np.ascontiguousarray(mem[c * n_seq:(c + 1) * n_seq].reshape(n_seq * NMEM, D))
        maps.append(m)
    return maps


_NC_CACHE = {}


def kernel(**inputs):
    n_cores, n_seq, tiles_per_seq = 8, 2, 4
    key = "full"
    if key not in _NC_CACHE:
        _NC_CACHE[key] = build(4, n_seq * tiles_per_seq, tiles_per_seq, ("mix", "attn", "mlp"), n_seq)
    nc = _NC_CACHE[key]
    maps = make_in_maps(inputs, n_cores, 4, n_seq * tiles_per_seq, tiles_per_seq, n_seq)
    r = run_bass_kernel_spmd(nc, maps, core_ids=list(range(n_cores)))
    outs = [np.asarray(r.results[c]["out"]).reshape(n_seq, tiles_per_seq * T, D) for c in range(n_cores)]
    return np.concatenate(outs, axis=0).astype(np.float32)
```
